# Optimizing a Trainium2 kernel written in Bass

```python
import math
import jax, jax.numpy as jnp
from jax import lax
import numpy as np

D_MODEL = 1024
BATCH = 32
SEQ = 2048
DEPTH = 4

GRID_W = 64
ROPE_THETA = 10000.0
Q_BLOCK = 128
EPS = 1e-6

MLA_HEADS = 16
MLA_NOPE = 64
MLA_ROPE = 32
MLA_QK = MLA_NOPE + MLA_ROPE
MLA_V = 64
Q_LORA = 384
KV_LORA = 256

GQA_HEADS = 16
GQA_KV_HEADS = 4
GQA_GROUP = GQA_HEADS // GQA_KV_HEADS
GQA_HD = D_MODEL // GQA_HEADS

D_FF = ((8 * D_MODEL + 3 * 256 - 1) // (3 * 256)) * 256

N_MIXERS = 2
N_MLA_LAYERS = (DEPTH + 1) // 2
N_GQA_LAYERS = DEPTH // 2

kernel_name = "interleaved_mla_gqa_axial_swiglu_encoder"


def rmsnorm(x, g):
    xf = x.astype(jnp.float32)
    y = xf * lax.rsqrt(jnp.mean(xf * xf, axis=-1, keepdims=True) + EPS)
    return (y * g.astype(jnp.float32)).astype(x.dtype)


def grid_positions(seq_len):
    rows = seq_len // GRID_W
    row = jnp.repeat(jnp.arange(rows, dtype=jnp.int32), GRID_W)
    col = jnp.tile(jnp.arange(GRID_W, dtype=jnp.int32), rows)
    return row, col


def rope_table(pos, dim):
    inv = ROPE_THETA ** (-jnp.arange(0, dim, 2, dtype=jnp.float32) / dim)
    ang = pos.astype(jnp.float32)[:, None] * inv[None, :]
    return jnp.cos(ang), jnp.sin(ang)


def axial_tables(row, col, rot_dim):
    half = rot_dim // 2
    cr, sr = rope_table(row, half)
    cc, sc = rope_table(col, half)
    return (cr, sr, cc, sc)


def rope_1d(x, cos, sin):
    d2 = x.shape[-1] // 2
    x1, x2 = x[..., :d2], x[..., d2:]
    c = cos[:, None, :]
    s = sin[:, None, :]
    return jnp.concatenate([x1 * c - x2 * s, x2 * c + x1 * s], axis=-1)


def axial_rope(x, tabs):
    cr, sr, cc, sc = tabs
    half = x.shape[-1] // 2
    xf = x.astype(jnp.float32)
    out = jnp.concatenate([rope_1d(xf[..., :half], cr, sr),
                           rope_1d(xf[..., half:], cc, sc)], axis=-1)
    return out.astype(x.dtype)


def blocked_attention(q, k, v, scale):
    B, S, HKV, G, Dk = q.shape
    nb = S // Q_BLOCK
    qb = q.reshape(B, nb, Q_BLOCK, HKV, G, Dk).transpose(1, 0, 2, 3, 4, 5)

    def one_block(q_blk):
        s = jnp.einsum('bqkgd,bskd->bkgqs', q_blk, k,
                       preferred_element_type=jnp.float32) * scale
        p = jax.nn.softmax(s, axis=-1).astype(v.dtype)
        return jnp.einsum('bkgqs,bskd->bqkgd', p, v)

    out = lax.map(one_block, qb)
    Dv = v.shape[-1]
    return out.transpose(1, 0, 2, 3, 4, 5).reshape(B, S, HKV * G, Dv)


def mla_mixer(h, w_in, q_lora_norm, w_uq, kv_lora_norm, w_ukv, q_norm, k_norm, w_o, tabs):
    B, S, _ = h.shape
    lat = h @ w_in
    c_q = rmsnorm(lat[..., :Q_LORA], q_lora_norm)
    c_kv = rmsnorm(lat[..., Q_LORA:Q_LORA + KV_LORA], kv_lora_norm)
    k_r = lat[..., Q_LORA + KV_LORA:][:, :, None, :]

    q = (c_q @ w_uq).reshape(B, S, MLA_HEADS, MLA_QK)
    kv = (c_kv @ w_ukv).reshape(B, S, MLA_HEADS, MLA_NOPE + MLA_V)
    k_nope, v = kv[..., :MLA_NOPE], kv[..., MLA_NOPE:]

    q_nope = rmsnorm(q[..., :MLA_NOPE], q_norm[:MLA_NOPE])
    q_rope = axial_rope(rmsnorm(q[..., MLA_NOPE:], q_norm[MLA_NOPE:]), tabs)
    k_nope = rmsnorm(k_nope, k_norm[:MLA_NOPE])
    k_rope = axial_rope(rmsnorm(k_r, k_norm[MLA_NOPE:]), tabs)
    k_rope = jnp.broadcast_to(k_rope, (B, S, MLA_HEADS, MLA_ROPE))

    q_full = jnp.concatenate([q_nope, q_rope], axis=-1)[:, :, :, None, :]
    k_full = jnp.concatenate([k_nope, k_rope], axis=-1)
    o = blocked_attention(q_full, k_full, v, MLA_QK ** -0.5)
    return o.reshape(B, S, MLA_HEADS * MLA_V) @ w_o


def gqa_mixer(h, w_qkv, q_norm, k_norm, w_o, tabs):
    B, S, _ = h.shape
    qkv = h @ w_qkv
    nq = GQA_HEADS * GQA_HD
    nk = GQA_KV_HEADS * GQA_HD
    q = qkv[..., :nq].reshape(B, S, GQA_HEADS, GQA_HD)
    k = qkv[..., nq:nq + nk].reshape(B, S, GQA_KV_HEADS, GQA_HD)
    v = qkv[..., nq + nk:].reshape(B, S, GQA_KV_HEADS, GQA_HD)
    q = axial_rope(rmsnorm(q, q_norm), tabs)
    k = axial_rope(rmsnorm(k, k_norm), tabs)
    q = q.reshape(B, S, GQA_KV_HEADS, GQA_GROUP, GQA_HD)
    o = blocked_attention(q, k, v, GQA_HD ** -0.5)
    return o.reshape(B, S, GQA_HEADS * GQA_HD) @ w_o


def swiglu(h, w_gate_up, w_down):
    gu = h @ w_gate_up
    g, u = gu[..., :D_FF], gu[..., D_FF:]
    return (jax.nn.silu(g) * u) @ w_down


def setup_inputs(seed: int = 0) -> dict:
    key = jax.random.key(seed)
    ks = jax.random.split(key, 20)

    def w(k, shape, fan_in):
        return jax.random.normal(k, shape, jnp.float32) * (fan_in ** -0.5)

    def gain(k, shape):
        return 1.0 + 0.02 * jax.random.normal(k, shape, jnp.float32)

    LA, LB, L = N_MLA_LAYERS, N_GQA_LAYERS, DEPTH
    return {
        "x": jax.random.normal(ks[0], (BATCH, SEQ, D_MODEL), jnp.float32),
        "mla_norm": gain(ks[1], (LA, D_MODEL)),
        "mla_w_in": w(ks[2], (LA, D_MODEL, Q_LORA + KV_LORA + MLA_ROPE), D_MODEL),
        "mla_q_lora_norm": gain(ks[3], (LA, Q_LORA)),
        "mla_w_uq": w(ks[4], (LA, Q_LORA, MLA_HEADS * MLA_QK), Q_LORA),
        "mla_kv_lora_norm": gain(ks[5], (LA, KV_LORA)),
        "mla_w_ukv": w(ks[6], (LA, KV_LORA, MLA_HEADS * (MLA_NOPE + MLA_V)), KV_LORA),
        "mla_q_norm": gain(ks[7], (LA, MLA_QK)),
        "mla_k_norm": gain(ks[8], (LA, MLA_QK)),
        "mla_w_o": w(ks[9], (LA, MLA_HEADS * MLA_V, D_MODEL), MLA_HEADS * MLA_V),
        "gqa_norm": gain(ks[10], (LB, D_MODEL)),
        "gqa_w_qkv": w(ks[11], (LB, D_MODEL, (GQA_HEADS + 2 * GQA_KV_HEADS) * GQA_HD), D_MODEL),
        "gqa_q_norm": gain(ks[12], (LB, GQA_HD)),
        "gqa_k_norm": gain(ks[13], (LB, GQA_HD)),
        "gqa_w_o": w(ks[14], (LB, GQA_HEADS * GQA_HD, D_MODEL), GQA_HEADS * GQA_HD),
        "ffn_norm": gain(ks[15], (L, D_MODEL)),
        "ffn_w_gate_up": w(ks[16], (L, D_MODEL, 2 * D_FF), D_MODEL),
        "ffn_w_down": w(ks[17], (L, D_FF, D_MODEL), D_FF),
    }


def reference(x, mla_norm, mla_w_in, mla_q_lora_norm, mla_w_uq, mla_kv_lora_norm,
              mla_w_ukv, mla_q_norm, mla_k_norm, mla_w_o, gqa_norm, gqa_w_qkv,
              gqa_q_norm, gqa_k_norm, gqa_w_o, ffn_norm, ffn_w_gate_up, ffn_w_down):
    S = x.shape[1]
    row, col = grid_positions(S)
    mla_tabs = axial_tables(row, col, MLA_ROPE)
    gqa_tabs = axial_tables(row, col, GQA_HD)

    for i in range(DEPTH):
        j = i // N_MIXERS
        if i % N_MIXERS == 0:
            h = rmsnorm(x, mla_norm[j])
            x = x + mla_mixer(h, mla_w_in[j], mla_q_lora_norm[j], mla_w_uq[j],
                              mla_kv_lora_norm[j], mla_w_ukv[j], mla_q_norm[j],
                              mla_k_norm[j], mla_w_o[j], mla_tabs)
        else:
            h = rmsnorm(x, gqa_norm[j])
            x = x + gqa_mixer(h, gqa_w_qkv[j], gqa_q_norm[j], gqa_k_norm[j],
                              gqa_w_o[j], gqa_tabs)
        h = rmsnorm(x, ffn_norm[i])
        x = x + swiglu(h, ffn_w_gate_up[i], ffn_w_down[i])
    return x
```

```python
import numpy as np
import ml_dtypes
from contextlib import ExitStack
import concourse.bass as bass
import concourse.mybir as mybir
from concourse.bass_utils import run_bass_kernel_spmd

F32 = mybir.dt.float32
BF16 = mybir.dt.bfloat16
ALU = mybir.AluOpType
AF = mybir.ActivationFunctionType

D = 1024
S = 2048
NCORE = 8
DFF = 2816
NJ = DFF // 128
EPS = 1e-6
DEPTH = 4
EH = [0, 1, 2, 3, 8, 9, 10, 11]
OH = [4, 5, 6, 7, 12, 13, 14, 15]

MLA_WIN = 0
MLA_WKR = MLA_WIN + 8 * 640
MLA_WH = MLA_WKR + 8 * 96
MLA_WO = MLA_WH + 16 * 544
MLA_END = MLA_WO + 8 * 1024
GQA_WQ = 0
GQA_WK = GQA_WQ + 8 * 1024
GQA_WV = GQA_WK + 2 * 1024
GQA_WO = GQA_WV + 8 * 256
GQA_END = GQA_WO + 8 * 1024
ATT_END = max(MLA_END, GQA_END)
FFN_GU = ATT_END
FFN_DN = FFN_GU + NJ * 2048
XL = FFN_DN + 8 * NJ * 128

def _pcols():
    off = 0
    cols = {}
    for j in range(2):
        for nm, n in (("mla_norm", 8), ("q_lora", 3), ("kv_lora", 2), ("mla_q", 1), ("mla_kn", 1), ("mla_kr", 1)):
            cols[(nm, j)] = off
            off += n
    for j in range(2):
        for nm, n in (("gqa_norm", 8), ("gqa_q", 1), ("gqa_k", 1)):
            cols[(nm, j)] = off
            off += n
    for i in range(4):
        cols[("ffn_norm", i)] = off
        off += 8
    return cols, off
PCOL, NPAR = _pcols()

C_ONES, C_B128, C_B96, C_R128, C_R96 = 0, 128, 256, 384, 512


def _host_consts():
    cm = np.zeros((128, 5 * 128), np.float32)
    cm[:, C_ONES:C_ONES + 128] = 1.0
    cm[0:64, C_B128:C_B128 + 64] = 1.0 / 64
    cm[64:128, C_B128 + 64:C_B128 + 128] = 1.0 / 64
    cm[0:64, C_B96:C_B96 + 64] = 1.0 / 64
    cm[64:96, C_B96 + 64:C_B96 + 96] = 1.0 / 32
    for hb in (0, 64):
        for blk in (0, 32):
            for i in range(32):
                dst = hb + blk + i
                if i < 16:
                    cm[dst + 16, C_R128 + dst] = -1.0
                else:
                    cm[dst - 16, C_R128 + dst] = 1.0
    for blk in (0, 16):
        for i in range(16):
            dst = 64 + blk + i
            if i < 8:
                cm[dst + 8, C_R96 + dst] = -1.0
            else:
                cm[dst - 8, C_R96 + dst] = 1.0
    t = np.arange(S)
    row = (t // 64).astype(np.float32)
    col = (t % 64).astype(np.float32)
    tab = np.zeros((2, 128, 2, S), np.float32)
    tab[0, :, 0, :] = 1.0
    inv8 = (np.float32(10000.0) ** (-np.arange(0, 16, 2, dtype=np.float32) / np.float32(16))).astype(np.float32)
    for blk, pos in ((0, row), (16, col)):
        for i in range(16):
            ang = (pos * inv8[i % 8]).astype(np.float32)
            tab[0, 64 + blk + i, 0, :] = np.cos(ang)
            tab[0, 64 + blk + i, 1, :] = np.sin(ang)
    inv16 = (np.float32(10000.0) ** (-np.arange(0, 32, 2, dtype=np.float32) / np.float32(32))).astype(np.float32)
    for hb in (0, 64):
        for blk, pos in ((0, row), (32, col)):
            for i in range(32):
                ang = (pos * inv16[i % 16]).astype(np.float32)
                tab[1, hb + blk + i, 0, :] = np.cos(ang)
                tab[1, hb + blk + i, 1, :] = np.sin(ang)
    return cm.astype(ml_dtypes.bfloat16), tab.astype(ml_dtypes.bfloat16)


def _host_layout(inp):
    wb = np.zeros((DEPTH, 128, XL), np.float32)
    par = np.zeros((128, NPAR), np.float32)

    def kc_slab(w, cols):
        k = w.shape[0] // 128
        return w[:, cols].reshape(k, 128, len(cols)).transpose(1, 0, 2).reshape(128, -1)

    for i in range(DEPTH):
        j = i // 2
        if i % 2 == 0:
            w_in = inp["mla_w_in"][j]
            wb[i, :, MLA_WIN:MLA_WIN + 8 * 640] = kc_slab(w_in, np.arange(640))
            kr = np.zeros((1024, 96), np.float32)
            kr[:, 64:96] = w_in[:, 640:672]
            wb[i, :, MLA_WKR:MLA_WKR + 8 * 96] = kc_slab(kr, np.arange(96))
            wuq = inp["mla_w_uq"][j]
            wukv = inp["mla_w_ukv"][j]
            for h in range(16):
                o = MLA_WH + h * 544
                wb[i, :, o:o + 288] = kc_slab(wuq, np.arange(h * 96, h * 96 + 96))
                wb[i, :, o + 288:o + 416] = kc_slab(wukv, np.arange(h * 128, h * 128 + 64))
                wb[i, :, o + 416:o + 544] = kc_slab(wukv, np.arange(h * 128 + 64, h * 128 + 128))
            wo = inp["mla_w_o"][j]
            for m in range(8):
                wb[i, :, MLA_WO + m * 1024:MLA_WO + (m + 1) * 1024] = kc_slab(wo, np.arange(m * 128, m * 128 + 128))
            par[:, PCOL[("mla_norm", j)]:PCOL[("mla_norm", j)] + 8] = inp["mla_norm"][j].reshape(8, 128).T
            par[:, PCOL[("q_lora", j)]:PCOL[("q_lora", j)] + 3] = inp["mla_q_lora_norm"][j].reshape(3, 128).T
            par[:, PCOL[("kv_lora", j)]:PCOL[("kv_lora", j)] + 2] = inp["mla_kv_lora_norm"][j].reshape(2, 128).T
            par[0:96, PCOL[("mla_q", j)]] = inp["mla_q_norm"][j]
            par[0:64, PCOL[("mla_kn", j)]] = inp["mla_k_norm"][j][:64]
            par[64:96, PCOL[("mla_kr", j)]] = inp["mla_k_norm"][j][64:96]
        else:
            wqkv = inp["gqa_w_qkv"][j]
            for c in range(8):
                cols = np.concatenate([np.arange(EH[c] * 64, EH[c] * 64 + 64), np.arange(OH[c] * 64, OH[c] * 64 + 64)])
                wb[i, :, GQA_WQ + c * 1024:GQA_WQ + (c + 1) * 1024] = kc_slab(wqkv, cols)
            for t in range(2):
                wb[i, :, GQA_WK + t * 1024:GQA_WK + (t + 1) * 1024] = kc_slab(wqkv, np.arange(1024 + t * 128, 1024 + t * 128 + 128))
            wb[i, :, GQA_WV:GQA_WV + 2048] = kc_slab(wqkv, np.arange(1280, 1536))
            wo = inp["gqa_w_o"][j]
            rows = np.concatenate([np.concatenate([np.arange(EH[c] * 64, EH[c] * 64 + 64), np.arange(OH[c] * 64, OH[c] * 64 + 64)]) for c in range(8)])
            wop = wo[rows, :]
            for m in range(8):
                wb[i, :, GQA_WO + m * 1024:GQA_WO + (m + 1) * 1024] = kc_slab(wop, np.arange(m * 128, m * 128 + 128))
            par[:, PCOL[("gqa_norm", j)]:PCOL[("gqa_norm", j)] + 8] = inp["gqa_norm"][j].reshape(8, 128).T
            par[:, PCOL[("gqa_q", j)]] = np.tile(inp["gqa_q_norm"][j], 2)
            par[:, PCOL[("gqa_k", j)]] = np.tile(inp["gqa_k_norm"][j], 2)
        wgu = inp["ffn_w_gate_up"][i]
        for jj in range(NJ):
            cols = np.concatenate([np.arange(jj * 128, jj * 128 + 128), np.arange(DFF + jj * 128, DFF + jj * 128 + 128)])
            wb[i, :, FFN_GU + jj * 2048:FFN_GU + (jj + 1) * 2048] = kc_slab(wgu, cols)
        wdn = inp["ffn_w_down"][i]
        for m in range(8):
            wb[i, :, FFN_DN + m * NJ * 128:FFN_DN + (m + 1) * NJ * 128] = kc_slab(wdn, np.arange(m * 128, m * 128 + 128))
        par[:, PCOL[("ffn_norm", i)]:PCOL[("ffn_norm", i)] + 8] = inp["ffn_norm"][i].reshape(8, 128).T
    return wb, par


class Buf:
    __slots__ = ("name", "w", "r", "excl")

    def __init__(self, name, excl=False):
        self.name = name
        self.w = None
        self.r = {}
        self.excl = excl


ENGS = ("pe", "act", "dve", "pool", "sp")
LIMIT = 16000


class Plan:
    def __init__(self):
        self.ops = {e: [] for e in ENGS}
        self.cnt = {e: 0 for e in ENGS}
        self.gen = {e: 0 for e in ENGS}
        self.waited = {e: {} for e in ENGS}
        self.dcnt = {}
        self.inckeys = []
        self._incset = set()

    def _addkey(self, k):
        if k not in self._incset:
            self._incset.add(k)
            self.inckeys.append(k)

    def _collect(self, reads, writes):
        deps = []
        for b in reads:
            if b.w is not None:
                deps.append(b.w)
        for b in writes:
            if b.w is not None:
                deps.append(b.w)
            for k, v in b.r.items():
                if k[0] == "e":
                    deps.append((("e", k[1], v[0]), v[1]))
                else:
                    deps.append((k, v))
        return deps

    def _need(self, eng, deps):
        out = {}
        wd = self.waited[eng]
        for key, val in deps:
            if key[0] == "e":
                if key[1] == "pe" and eng == "pe":
                    continue
                k2 = ("e", key[1])
                cur = wd.get(k2)
                if cur is not None and cur >= (key[2], val):
                    continue
                prev = out.get(k2)
                if prev is None or (key[2], val) > prev:
                    out[k2] = (key[2], val)
            else:
                if wd.get(key, 0) >= val:
                    continue
                if out.get(key, 0) < val:
                    out[key] = val
        waits = []
        for k, v in out.items():
            wd[k] = v
            if k[0] == "e":
                waits.append((("e", k[1], v[0]), v[1]))
            else:
                waits.append((k, v))
        return waits

    def _mark(self, tk, reads, writes):
        key, val = tk
        for b in reads:
            if key[0] == "e":
                k2 = ("e", key[1])
                nv = (key[2], val)
                if b.r.get(k2, (-1, -1)) < nv:
                    b.r[k2] = nv
            else:
                if b.r.get(key, 0) < val:
                    b.r[key] = val
        for b in writes:
            b.w = tk
            b.r = {}

    def op(self, eng, fn, reads=(), writes=(), signal=True):
        if any(b.excl for b in reads):
            writes = list(writes) + [b for b in reads if b.excl]
            reads = [b for b in reads if not b.excl]
        waits = self._need(eng, self._collect(reads, writes))
        if signal:
            if self.cnt[eng] >= LIMIT:
                self.gen[eng] += 1
                self.cnt[eng] = 0
            self.cnt[eng] += 1
            key = ("e", eng, self.gen[eng])
            tk = (key, self.cnt[eng])
            self._addkey(key)
            self.ops[eng].append((waits, fn, key, 1))
        else:
            g, c = self.gen[eng], self.cnt[eng]
            if c >= LIMIT:
                g += 1
                c = 0
            tk = (("e", eng, g), c + 1)
            self.ops[eng].append((waits, fn, None, 0))
        self._mark(tk, reads, writes)
        return tk

    def dma(self, q, fn, semid, reads=(), writes=()):
        waits = self._need(q, self._collect(reads, writes))
        key = ("d", semid)
        self.dcnt[key] = self.dcnt.get(key, 0) + 16
        tk = (key, self.dcnt[key])
        self._addkey(key)
        self.ops[q].append((waits, fn, key, 16))
        self._mark(tk, reads, writes)
        return tk

    def last_tickets(self):
        tks = []
        for e in ("pe", "act", "dve", "pool"):
            if self.gen[e] > 0 or self.cnt[e] > 0:
                tks.append((("e", e, self.gen[e]), self.cnt[e]))
        return tks

    def barrier(self, extra=()):
        tks = self.last_tickets() + list(extra)
        for e in ENGS:
            waits = self._need(e, tks)
            if waits:
                self.ops[e].append((waits, None, None, 0))

    def wait_only(self, eng, tks):
        waits = self._need(eng, tks)
        if waits:
            self.ops[eng].append((waits, None, None, 0))


class Sprinkle:
    def __init__(self, lanes, every=1):
        self.lanes = [list(l) for l in lanes]
        self.every = every
        self.i = 0

    def tick(self):
        self.i += 1
        if self.i % self.every == 0:
            self.step()

    def step(self):
        for l in self.lanes:
            while l:
                try:
                    next(l[0])
                    break
                except StopIteration:
                    l.pop(0)

    def drain(self):
        while any(self.lanes):
            self.step()


DBG_STOP = 99
FILLER = 2
NLANES = 3
INCR_NORM = True
POOL = "dve"


def build_nc(n_seq=4, layers=(0, 1, 2, 3), dbg=False):
    nc = bass.Bass("TRN2", target_bir_lowering=False)
    xin = nc.dram_tensor("xT", [n_seq, D, S], F32, kind="ExternalInput").ap()
    wfp = nc.dram_tensor("wblob", [DEPTH, 128, XL], F32, kind="ExternalInput").ap()
    par_d = nc.dram_tensor("params", [128, NPAR], F32, kind="ExternalInput").ap()
    cm_d = nc.dram_tensor("cmat", [128, 640], BF16, kind="ExternalInput").ap()
    tab_d = nc.dram_tensor("tabs", [2, 128, 2 * S], BF16, kind="ExternalInput").ap()
    wbf = nc.dram_tensor("wbf", [DEPTH, 128, XL], BF16, kind="Internal").ap()
    yout = nc.dram_tensor("yT", [n_seq, D, S], F32, kind="ExternalOutput").ap()

    P = Plan()
    with ExitStack() as es:
        def sb(name, shape, dt):
            return es.enter_context(nc.sbuf_tensor(name, shape, dt))

        def ps(name, shape):
            return es.enter_context(nc.psum_tensor(name, shape, F32))

        xT = sb("xT_sb", [128, 8 * S], F32)
        A = sb("A_sb", [128, 8 * S], BF16)
        BU = sb("BU_sb", [128, 32768], BF16)
        T32 = sb("T32_sb", [128, 4096], F32)
        TAB = sb("TAB_sb", [128, 2 * S], BF16)
        WT = sb("WT_sb", [128, 9280], BF16)
        CM = sb("CM_sb", [128, 640], BF16)
        PAR = sb("PAR_sb", [128, NPAR], F32)
        psS = [ps("psS0", [128, 1024]), ps("psS1", [128, 1024])]
        psO = [ps("psO0", [128, 512]), ps("psO1", [128, 512])]
        psP = [ps("psP1", [128, 512]), ps("psP2", [128, 512])]

        xb = [[Buf(f"x{kc}_{tt}") for tt in range(4)] for kc in range(8)]
        Ab = [[Buf(f"A{kc}_{tt}") for tt in range(4)] for kc in range(8)]
        Bb = [[Buf(f"B{kc}_{tt}") for tt in range(4)] for kc in range(8)]
        bS = [[Buf("S0a", True), Buf("S0b", True)], [Buf("S1a", True), Buf("S1b", True)]]
        bO = [Buf("O0", True), Buf("O1", True)]
        bP = [Buf("P1", True), Buf("P2", True)]
        tabb = Buf("tab")
        cmb = Buf("cm")
        parb = Buf("par")
        sqb = [Buf("sq0"), Buf("sq1")]
        qgb = [Buf("qg0"), Buf("qg1")]
        rstdb = [Buf("rstd0"), Buf("rstd1")]
        ab = [Buf("a0"), Buf("a1")]
        bb = [Buf("b0"), Buf("b1")]
        rdenb = [Buf("rden0"), Buf("rden1")]
        ptb = [Buf("pt0"), Buf("pt1")]
        wreg = {}

        def xv(kc, tt):
            return xT[:, kc * S + tt * 512: kc * S + (tt + 1) * 512]

        def Av(kc, t0, n, r0=0, r1=128):
            return A[r0:r1, kc * S + t0: kc * S + t0 + n]

        def Bv(kc, t0, n, r0=0, r1=128):
            return BU[r0:r1, kc * S + t0: kc * S + t0 + n]

        U0 = 16384
        PT = [BU[:, U0 + 12288 + s * 1024: U0 + 12288 + (s + 1) * 1024] for s in range(2)]
        SQ = [BU[:, U0 + 14336 + s * 512: U0 + 14336 + (s + 1) * 512] for s in range(2)]
        QG = [BU[:, U0 + 15360 + s * 512: U0 + 15360 + (s + 1) * 512] for s in range(2)]
        RSTD = [T32[:, s * 512:(s + 1) * 512] for s in range(2)]
        TA = [T32[:, 1024 + s * 512: 1024 + (s + 1) * 512] for s in range(2)]
        TB = [T32[:, 2048 + s * 512: 2048 + (s + 1) * 512] for s in range(2)]
        RDEN = [T32[:, 3072 + s * 512: 3072 + (s + 1) * 512] for s in range(2)]
        ones_ap = CM[:, C_ONES:C_ONES + 128]
        if NLANES == 3:
            TB2 = sb("TB2_sb", [128, 512], F32)
            SQ.append(BU[:, U0 + 12288: U0 + 12288 + 512])
            QG.append(BU[:, U0 + 12800: U0 + 12800 + 512])
            sqb.append(ptb[0])
            qgb.append(ptb[0])
            RSTD.append(RDEN[0])
            rstdb.append(rdenb[0])
            TA.append(RDEN[1])
            ab.append(rdenb[1])
            TB.append(TB2[:, :])
            bb.append(Buf("b2"))

        def cosv(t0, n, R):
            return TAB[0:R, t0:t0 + n]

        def sinv(t0, n, R):
            return TAB[0:R, S + t0:S + t0 + n]

        def pcol(name, idx, k=0, R=128, r0=0):
            c = PCOL[(name, idx)] + k
            return PAR[r0:R, c:c + 1]

        def mm(out, lhsT, rhs, start, stop, reads, writes, signal):
            P.op("pe", lambda e: e.matmul(out, lhsT=lhsT, rhs=rhs, start=start, stop=stop), reads, writes, signal)

        def act(out, in_, func, reads, writes, scale=1.0, bias=0.0):
            P.op("act", lambda e: e.activation(out=out, in_=in_, func=func, bias=bias, scale=scale), reads, writes)

        def tt_(eng, out, in0, in1, op, reads, writes):
            P.op(eng, lambda e: e.tensor_tensor(out=out, in0=in0, in1=in1, op=op), reads, writes)

        def ts_(eng, out, in0, scalar, op, reads, writes):
            P.op(eng, lambda e: e.tensor_scalar(out=out, in0=in0, scalar1=scalar, scalar2=None, op0=op), reads, writes)

        def stt_(eng, out, in0, scalar, in1, reads, writes):
            P.op(eng, lambda e: e.scalar_tensor_tensor(out=out, in0=in0, scalar=scalar, in1=in1, op0=ALU.mult, op1=ALU.mult), reads, writes)

        def copy_(eng, out, in_, reads, writes):
            P.op(eng, lambda e: e.tensor_copy(out=out, in_=in_), reads, writes)

        def dma(q, out, in_, semid, reads, writes):
            P.dma(q, lambda e: e.dma_start(out=out, in_=in_), semid, reads, writes)

        dma("sp", CM[:], cm_d, "cm", [], [cmb])
        dma("sp", PAR[:], par_d, "par", [], [parb])
        CH = 4096
        for li in layers:
            for (rname, c0, c1) in (("att", 0, ATT_END), ("gu", FFN_GU, FFN_DN), ("dn", FFN_DN, XL)):
                rb = Buf(f"w{li}{rname}")
                wreg[(li, rname)] = rb
                c = c0
                while c < c1:
                    n = min(CH, c1 - c)
                    dma("pool", wbf[li, :, c:c + n], wfp[li, :, c:c + n], f"cast{li}{rname}", [], [])
                    c += n
                key = ("d", f"cast{li}{rname}")
                rb.w = (key, P.dcnt[key])

        def rmsnorm_main(gname, gidx):
            for tt in range(4):
                for kc in range(8):
                    s = kc % 2
                    act(SQ[s], xv(kc, tt), AF.Square, [xb[kc][tt]], [sqb[s]])
                    mm(psP[1][:, :], ones_ap, SQ[s], kc == 0, kc == 7, [sqb[s], cmb], [bP[1]], True)
                act(RSTD[0], psP[1][:, :], AF.Ln, [bP[1]], [rstdb[0]], scale=1.0 / D, bias=EPS)
                act(RSTD[0], RSTD[0], AF.Exp, [rstdb[0]], [rstdb[0]], scale=-0.5)
                for kc in range(8):
                    stt_("dve", Av(kc, tt * 512, 512), xv(kc, tt), pcol(gname, gidx, kc), RSTD[0],
                         [xb[kc][tt], rstdb[0], parb], [Ab[kc][tt]])

        T32b = T32.bitcast(BF16)
        NSQ = [T32b[:, 6144 + s_ * 512: 6144 + (s_ + 1) * 512] for s_ in range(4)]
        nsqb = [Buf(f"nsq{s_}") for s_ in range(4)]
        NRSTD = [RSTD[0], RSTD[1], TB[0], TB[1]]
        nrstdb = [rstdb[0], rstdb[1], bb[0], bb[1]]
        nsq_i = [0]
        have_rstd = [False]

        def norm_stat(m, tt, bank_ap, bank_buf):
            s_ = nsq_i[0] % 4
            nsq_i[0] += 1
            act(NSQ[s_], xv(m, tt), AF.Square, [xb[m][tt]], [nsqb[s_], rdenb[s_ // 2]])
            mm(bank_ap, ones_ap, NSQ[s_], m == 0, m == 7, [nsqb[s_], rdenb[s_ // 2], cmb], [bank_buf], True)

        def norm_finish(tt, bank_ap, bank_buf):
            act(NRSTD[tt], bank_ap, AF.Ln, [bank_buf], [nrstdb[tt]], scale=1.0 / D, bias=EPS)
            act(NRSTD[tt], NRSTD[tt], AF.Exp, [nrstdb[tt]], [nrstdb[tt]], scale=-0.5)

        def norm_apply_one(gname, gidx, tt, kc):
            stt_("dve", Av(kc, tt * 512, 512), xv(kc, tt), pcol(gname, gidx, kc), NRSTD[tt],
                 [xb[kc][tt], nrstdb[tt], parb], [Ab[kc][tt]])

        def do_norm(gname, gidx, defer_from=4):
            later = []
            if have_rstd[0]:
                for tt in range(4):
                    for kc in range(8):
                        if tt < defer_from:
                            norm_apply_one(gname, gidx, tt, kc)
                        else:
                            later.append(lambda tt=tt, kc=kc: norm_apply_one(gname, gidx, tt, kc))
                have_rstd[0] = False
            else:
                rmsnorm_main(gname, gidx)
            return later

        def head_tile(pbank, pbuf, R, gain_ap, bones_ap, rot_ap, t0, outs, slot, proj_fn, qg_act=False):
            proj_fn()
            yield
            ps_ap = pbank[0:R, :]
            act(SQ[slot][0:R, :], ps_ap, AF.Square, [pbuf], [sqb[slot]])
            yield
            if qg_act:
                P.op("act", lambda e: e.activation(out=QG[slot][0:R, :], in_=ps_ap, func=AF.Identity, bias=0.0, scale=gain_ap),
                     [pbuf, parb], [qgb[slot]])
            else:
                ts_("dve", QG[slot][0:R, :], ps_ap, gain_ap, ALU.mult, [pbuf, parb], [qgb[slot]])
            yield
            mm(ps_ap, bones_ap, SQ[slot][0:R, :], True, True, [sqb[slot], cmb], [pbuf], True)
            yield
            act(RSTD[slot][0:R, :], ps_ap, AF.Ln, [pbuf], [rstdb[slot]], scale=1.0, bias=EPS)
            act(RSTD[slot][0:R, :], RSTD[slot][0:R, :], AF.Exp, [rstdb[slot]], [rstdb[slot]], scale=-0.5)
            yield
            if rot_ap is not None:
                mm(ps_ap, rot_ap, QG[slot][0:R, :], True, True, [qgb[slot], cmb], [pbuf], True)
                tt_("dve", TA[slot][0:R, :], QG[slot][0:R, :], cosv(t0, 512, R), ALU.mult, [qgb[slot], tabb], [ab[slot]])
                yield
                tt_("dve", TB[slot][0:R, :], ps_ap, sinv(t0, 512, R), ALU.mult, [pbuf, tabb], [bb[slot]])
                tt_("dve", TA[slot][0:R, :], TA[slot][0:R, :], TB[slot][0:R, :], ALU.add, [ab[slot], bb[slot]], [ab[slot]])
                for (oap, obufs, r0, r1) in outs:
                    tt_("dve", oap, TA[slot][r0:r1, :], RSTD[slot][r0:r1, :], ALU.mult, [ab[slot], rstdb[slot]], obufs)
            else:
                for (oap, obufs, r0, r1) in outs:
                    tt_("dve", oap, QG[slot][r0:r1, :], RSTD[slot][r0:r1, :], ALU.mult, [qgb[slot], rstdb[slot]], obufs)
            yield

        def pe_filler(n):
            for _ in range(n):
                P.op("pe", lambda e: e.matmul(psO[1][:, :], lhsT=ones_ap, rhs=TAB[:, 0:512], start=True, stop=True),
                     [tabb, cmb], [bO[1]], False)

        def run_sched(items, nl, filler=0):
            items = list(items)
            active = [None] * nl
            while items or any(a is not None for a in active):
                for l in range(nl):
                    if active[l] is None:
                        while items and items[0][0] == "load":
                            items.pop(0)[1]()
                        if items:
                            active[l] = items.pop(0)[1](l)
                for l in range(nl):
                    if active[l] is not None:
                        try:
                            next(active[l])
                        except StopIteration:
                            active[l] = None
                if filler:
                    pe_filler(filler)

        def run_lanes(lanes):
            lanes = [list(l) for l in lanes]
            while any(lanes):
                for l in lanes:
                    while l:
                        try:
                            next(l[0])
                            break
                        except StopIteration:
                            l.pop(0)

        def run(gen):
            for _ in gen:
                pass

        deferred = []

        def push_norm(obank, obuf, rden, rdbuf, o0, oap, obufs):
            d0 = 64 - o0
            for c0 in range(0, 512, 128):
                deferred.append(lambda c0=c0: P.op(
                    "dve", (lambda e: e.reciprocal(out=rden[o0:o0 + 64, c0:c0 + 128], in_=obank[d0:d0 + 64, c0:c0 + 128])),
                    [obuf], [rdbuf]))
            deferred.append(lambda: tt_("dve", oap, obank[o0:o0 + 64, :], rden[o0:o0 + 64, :], ALU.mult, [obuf, rdbuf], obufs))

        def pop_deferred():
            if deferred:
                deferred.pop(0)()

        def flush_deferred():
            while deferred:
                deferred.pop(0)()

        def attention(units, sprinkle, scale):
            iters = [(u, qc, kp) for u in units for qc in range(4) for kp in range(8)]

            def qk(i):
                u, qc, kp = iters[i]
                s = i % 2
                for hh in range(2):
                    kt = 2 * kp + hh
                    mm(psS[s][:, hh * 512:(hh + 1) * 512], u["k"](kt), u["q"](qc), True, True,
                       u["kb"](kt) + u["qb"](qc), [bS[s][hh]], hh == 1)

            qk(0)
            for i, (u, qc, kp) in enumerate(iters):
                s = i % 2
                if i + 1 < len(iters):
                    qk(i + 1)
                act(PT[s], psS[s][:, :], AF.Exp, [bS[s][0], bS[s][1]], [ptb[s]], scale=scale)
                n = i // 8
                ob = n % 2
                for hh in range(2):
                    kt = 2 * kp + hh
                    mm(psO[ob][:, :], u["v"](kt), PT[s][:, hh * 512:(hh + 1) * 512],
                       kp == 0 and hh == 0, kp == 7 and hh == 1, [ptb[s]] + u["vb"], [bO[ob]], hh == 1)
                if kp == 7:
                    o0 = 0 if u["top"] else 64
                    oap, obufs = u["out"](qc)
                    push_norm(psO[ob], bO[ob], RDEN[ob], rdenb[ob], o0, oap, obufs)
                pop_deferred()
                sprinkle.tick()
            sprinkle.drain()

        WO_SLOT = [WT[:, s * 1024:(s + 1) * 1024] for s in range(2)]
        wob = [Buf("wo0"), Buf("wo1")]

        def wo_phase(li, wo_off, src_v, src_b):
            pend = []
            for m in range(8):
                s = m % 2
                dma("sp", WO_SLOT[s], wbf[li, :, wo_off + m * 1024: wo_off + (m + 1) * 1024], f"wo{s}",
                    [wreg[(li, "att")]], [wob[s]])
                for tt in range(4):
                    ob = tt % 2
                    for c in range(8):
                        mm(psO[ob][:, :], WO_SLOT[s][:, c * 128:(c + 1) * 128], src_v(c, tt * 512, 512), c == 0, c == 7,
                           [wob[s], src_b[c][tt]], [bO[ob]], c == 7)
                    tt_("dve", xv(m, tt), xv(m, tt), psO[ob][:, :], ALU.add, [xb[m][tt], bO[ob]], [xb[m][tt]])
                    if INCR_NORM:
                        pend.append((m, tt, psS[tt // 2][:, (tt % 2) * 512:(tt % 2 + 1) * 512], bS[tt // 2][tt % 2]))
                        if len(pend) > 2:
                            norm_stat(*pend.pop(0))
            if INCR_NORM:
                while pend:
                    norm_stat(*pend.pop(0))
                for tt in range(4):
                    norm_finish(tt, psS[tt // 2][:, (tt % 2) * 512:(tt % 2 + 1) * 512], bS[tt // 2][tt % 2])
                have_rstd[0] = True

        ACTT = lambda j, t0, n: BU[:, j * 1024 + t0: j * 1024 + t0 + n]
        WGU = [BU[:, 22528 + s * 2048: 22528 + (s + 1) * 2048] for s in range(2)]
        WDN = [BU[:, 26624 + s * 2816: 26624 + (s + 1) * 2816] for s in range(2)]
        wgub = [Buf("wgu0"), Buf("wgu1")]
        wdnb = [Buf("wdn0"), Buf("wdn1")]
        actb = [[Buf(f"act{j}_{t}") for t in range(2)] for j in range(NJ)]

        def ffn(li, stats, store_seq):
            pend = []
            later = do_norm("ffn_norm", li, defer_from=2)
            for half in range(2):
                for j in range(NJ):
                    if later:
                        later.pop(0)()
                    s = j % 2
                    dma("sp", WGU[s], wbf[li, :, FFN_GU + j * 2048: FFN_GU + (j + 1) * 2048], f"wgu{s}",
                        [wreg[(li, "gu")]], [wgub[s]])
                    for t2 in range(2):
                        tt = half * 2 + t2
                        for kc in range(8):
                            mm(psS[0][:, t2 * 512:(t2 + 1) * 512], WGU[s][:, kc * 256: kc * 256 + 128], Av(kc, tt * 512, 512),
                               kc == 0, kc == 7, [wgub[s], Ab[kc][tt]], [bS[0][t2]], kc == 7)
                        for kc in range(8):
                            mm(psS[1][:, t2 * 512:(t2 + 1) * 512], WGU[s][:, kc * 256 + 128: kc * 256 + 256], Av(kc, tt * 512, 512),
                               kc == 0, kc == 7, [wgub[s], Ab[kc][tt]], [bS[1][t2]], kc == 7)
                        act(TA[t2], psS[0][:, t2 * 512:(t2 + 1) * 512], AF.Silu, [bS[0][t2]], [ab[t2]])
                        tt_("dve", ACTT(j, t2 * 512, 512), TA[t2], psS[1][:, t2 * 512:(t2 + 1) * 512], ALU.mult,
                            [ab[t2], bS[1][t2]], [actb[j][t2]])
                while later:
                    later.pop(0)()
                for m in range(8):
                    s = m % 2
                    dma("sp", WDN[s], wbf[li, :, FFN_DN + m * 2816: FFN_DN + (m + 1) * 2816], f"wdn{s}",
                        [wreg[(li, "dn")]], [wdnb[s]])
                    for t2 in range(2):
                        tt = half * 2 + t2
                        for j in range(NJ):
                            mm(psO[t2][:, :], WDN[s][:, j * 128:(j + 1) * 128], ACTT(j, t2 * 512, 512), j == 0, j == NJ - 1,
                               [wdnb[s], actb[j][t2]], [bO[t2]], j == NJ - 1)
                        tt_("dve", xv(m, tt), xv(m, tt), psO[t2][:, :], ALU.add, [xb[m][tt], bO[t2]], [xb[m][tt]])
                        if stats:
                            pend.append((m, tt, psP[t2][:, :], bP[t2]))
                            if len(pend) > 2:
                                norm_stat(*pend.pop(0))
                    if store_seq is not None and half == 1:
                        dma("act", yout[store_seq, m * 128:(m + 1) * 128, :], xT[:, m * S:(m + 1) * S], f"xs{m}", xb[m], [])
                if stats:
                    while pend:
                        norm_stat(*pend.pop(0))
                    for t2 in range(2):
                        norm_finish(half * 2 + t2, psP[t2][:, :], bP[t2])
            if stats:
                have_rstd[0] = True

        GK = [BU[:, U0 + i * 2048: U0 + (i + 1) * 2048] for i in range(2)]
        gkb = [[Buf(f"gk{i}_{tt}") for tt in range(4)] for i in range(2)]
        GV0 = U0 + 4096
        gvb = [Buf(f"gv{j}") for j in range(4)]
        GW = [WT[:, 2048 + s * 1024: 2048 + (s + 1) * 1024] for s in range(3)]
        gwb = [Buf(f"gw{s}") for s in range(3)]
        GWV = WT[:, 5120:5120 + 2048]
        gwvb = Buf("gwv")

        def gqa(li):
            j = li // 2
            dma("sp", TAB[:], tab_d[1], "tab", [], [tabb])
            do_norm("gqa_norm", j)
            P.op(POOL, lambda e: e.memset(BU[:, GV0:GV0 + 8192], 1.0), [], gvb)
            wcount = [0]

            def load_w(off):
                s = wcount[0] % 3
                wcount[0] += 1
                dma("sp", GW[s], wbf[li, :, off:off + 1024], f"gw{s}", [wreg[(li, "att")]], [gwb[s]])
                return s

            lane_banks = [(psP[0], bP[0]), (psP[1], bP[1]), (psO[0], bO[0])]
            ws_map = {}

            def loadw_item(off, key):
                def th():
                    ws_map[key] = load_w(off)
                return ("load", th)

            def tile_item(key, tt, gain_ap, outs):
                def factory(lane):
                    pbank, pbuf = lane_banks[lane]

                    def proj():
                        ws = ws_map[key]
                        for kc in range(8):
                            mm(pbank[:, :], GW[ws][:, kc * 128:(kc + 1) * 128], Av(kc, tt * 512, 512), kc == 0, kc == 7,
                               [gwb[ws], Ab[kc][tt]], [pbuf], kc == 7)
                    return head_tile(pbank, pbuf, 128, gain_ap, CM[:, C_B128:C_B128 + 128], CM[:, C_R128:C_R128 + 128],
                                     tt * 512, outs, lane, proj, qg_act=True)
                return ("tile", factory)

            kitems = [loadw_item(GQA_WK, ("k", 0)), loadw_item(GQA_WK + 1024, ("k", 1))]
            for t in range(2):
                for tt in range(4):
                    kitems.append(tile_item(("k", t), tt, pcol("gqa_k", j),
                                            [(GK[t][:, tt * 512:(tt + 1) * 512], [gkb[t][tt]], 0, 128)]))
            run_sched(kitems, NLANES, FILLER)
            dma("sp", GWV, wbf[li, :, GQA_WV:GQA_WV + 2048], "gwv", [wreg[(li, "att")]], [gwvb])
            for t16 in range(16):
                bank = psS[(t16 // 2) % 2]
                hb = t16 % 2
                bbuf = bS[(t16 // 2) % 2][hb]
                for kc in range(8):
                    mm(bank[:, hb * 512: hb * 512 + 256], Av(kc, t16 * 128, 128), GWV[:, kc * 256:(kc + 1) * 256], kc == 0, kc == 7,
                       [gwvb, Ab[kc][t16 // 4]], [bbuf], kc == 7)
                for kv in range(4):
                    dst = GV0 + kv * 2048 + t16 * 128 + (kv % 2) * 64
                    copy_("dve", BU[:, dst:dst + 64], bank[:, hb * 512 + kv * 64: hb * 512 + kv * 64 + 64], [bbuf], [gvb[kv]])
            qitems = [loadw_item(GQA_WQ, ("q", 0))]
            for c in range(8):
                if c + 1 < 8:
                    qitems.append(loadw_item(GQA_WQ + (c + 1) * 1024, ("q", c + 1)))
                for tt in range(4):
                    qitems.append(tile_item(("q", c), tt, pcol("gqa_q", j),
                                            [(Bv(c, tt * 512, 512), [Bb[c][tt]], 0, 128)]))
            run_sched(qitems, NLANES, FILLER)
            iters = [(c, qc, kt) for c in range(8) for qc in range(4) for kt in range(16)]
            obanks = [(psO[0], bO[0]), (psO[1], bO[1]), (psP[0], bP[0]), (psP[1], bP[1])]

            def qk(i):
                c, qc, kt = iters[i]
                s = i % 2
                t = 0 if c < 4 else 1
                for hh in range(2):
                    r0 = hh * 64
                    mm(psS[s][:, hh * 512:(hh + 1) * 512], GK[t][r0:r0 + 64, kt * 128:(kt + 1) * 128],
                       Bv(c, qc * 512, 512, r0, r0 + 64), True, True, [gkb[t][kt // 4], Bb[c][qc]], [bS[s][hh]], hh == 1)

            qk(0)
            for i, (c, qc, kt) in enumerate(iters):
                s = i % 2
                if i + 1 < len(iters):
                    qk(i + 1)
                act(PT[s], psS[s][:, :], AF.Exp, [bS[s][0], bS[s][1]], [ptb[s]], scale=64 ** -0.5)
                u = (i // 16) % 2
                for hh in range(2):
                    kv = (EH[c] if hh == 0 else OH[c]) // 4
                    ob, obuf = obanks[2 * u + hh]
                    vo = GV0 + kv * 2048 + kt * 128
                    mm(ob[:, :], BU[:, vo:vo + 128], PT[s][:, hh * 512:(hh + 1) * 512], kt == 0, kt == 15,
                       [ptb[s], gvb[kv]], [obuf], hh == 1)
                if kt == 15:
                    for hh in range(2):
                        ob, obuf = obanks[2 * u + hh]
                        push_norm(ob, obuf, RDEN[hh], rdenb[hh], hh * 64, Av(c, qc * 512, 512, hh * 64, hh * 64 + 64), [Ab[c][qc]])
                pop_deferred()
            flush_deferred()
            wo_phase(li, GQA_WO, lambda c, t0, n: Av(c, t0, n), Ab)

        MQ = [BU[:, U0 + s * 2048: U0 + (s + 1) * 2048] for s in range(2)]
        MK = [BU[:, U0 + 4096 + s * 2048: U0 + 4096 + (s + 1) * 2048] for s in range(2)]
        MV0 = U0 + 8192
        mqb = [[Buf(f"mq{s}_{tt}") for tt in range(4)] for s in range(2)]
        mkb = [[Buf(f"mk{s}_{tt}") for tt in range(4)] for s in range(2)]
        mvb = [Buf("mv0"), Buf("mv1")]
        WIN = WT[:, 2048:2048 + 5120]
        WKR = WT[:, 7168:7168 + 768]
        winb = Buf("win")
        HW = [WT[:, 7936 + s * 544: 7936 + (s + 1) * 544] for s in range(2)]
        hwb = [Buf("hw0"), Buf("hw1")]

        def mla(li):
            j = li // 2
            if DBG_STOP < 1:
                return
            dma("sp", TAB[:], tab_d[0], "tab", [], [tabb])
            dma("sp", WT[:, 2048:2048 + 5888], wbf[li, :, MLA_WIN:MLA_WIN + 5888], "win", [wreg[(li, "att")]], [winb])
            mla_later = do_norm("mla_norm", j)
            P.op(POOL, lambda e: e.memset(BU[:, MV0:MV0 + 4096], 1.0), [], mvb)
            for s_ in range(2):
                P.op("dve", (lambda e, s_=s_: e.memset(MQ[s_][64:128, :], 0.0)), [], mqb[s_])
                P.op("dve", (lambda e, s_=s_: e.memset(MK[s_][64:128, :], 0.0)), [], mkb[s_])
            banks = [(psS[0][:, 0:512], bS[0][0]), (psS[0][:, 512:1024], bS[0][1]), (psS[1][:, 0:512], bS[1][0]),
                     (psS[1][:, 512:1024], bS[1][1]), (psO[0][:, :], bO[0])]
            for tt in range(4):
                for _ in range(8):
                    if mla_later:
                        mla_later.pop(0)()
                hT_b = [Ab[kc][tt] for kc in range(8)]
                for ch in range(5):
                    bk, bkb = banks[ch]
                    for kc in range(8):
                        mm(bk, WIN[:, kc * 640 + ch * 128: kc * 640 + (ch + 1) * 128], Av(kc, tt * 512, 512), kc == 0, kc == 7,
                           [winb, Ab[kc][tt]], [bkb], kc == 7)
                    s = ch % 2
                    act(SQ[s], bk, AF.Square, [bkb], [sqb[s]])
                    if ch < 3:
                        mm(psP[0][:, :], ones_ap, SQ[s], ch == 0, ch == 2, [sqb[s], cmb], [bP[0]], True)
                    else:
                        mm(psP[1][:, :], ones_ap, SQ[s], ch == 3, ch == 4, [sqb[s], cmb], [bP[1]], True)
                for kc in range(8):
                    mm(psO[1][0:96, :], WKR[:, kc * 96:(kc + 1) * 96], Av(kc, tt * 512, 512), kc == 0, kc == 7,
                       [winb, Ab[kc][tt]], [bO[1]], kc == 7)
                act(RSTD[0], psP[0][:, :], AF.Ln, [bP[0]], [rstdb[0]], scale=1.0 / 384, bias=EPS)
                act(RSTD[0], RSTD[0], AF.Exp, [rstdb[0]], [rstdb[0]], scale=-0.5)
                act(RSTD[1], psP[1][:, :], AF.Ln, [bP[1]], [rstdb[1]], scale=1.0 / 256, bias=EPS)
                act(RSTD[1], RSTD[1], AF.Exp, [rstdb[1]], [rstdb[1]], scale=-0.5)
                for ch in range(5):
                    bk, bkb = banks[ch]
                    if ch < 3:
                        g_ap, r = pcol("q_lora", j, ch), 0
                    else:
                        g_ap, r = pcol("kv_lora", j, ch - 3), 1
                    stt_("dve", Av(ch, tt * 512, 512), bk, g_ap, RSTD[r], [bkb, rstdb[r], parb] , [Ab[ch][tt]] + ([] if ch else hT_b[5:]))
                run(head_tile(psO[1], bO[1], 96, pcol("mla_kr", j, 0, 96), CM[0:96, C_B96:C_B96 + 96],
                              CM[0:96, C_R96:C_R96 + 96], tt * 512,
                              [(MK[0][64:96, tt * 512:(tt + 1) * 512], [mkb[0][tt]], 64, 96),
                               (MK[1][64:96, tt * 512:(tt + 1) * 512], [mkb[1][tt]], 64, 96)], 0, lambda: None))

            def load_hw(h):
                s = h % 2
                o = MLA_WH + h * 544
                dma("sp", HW[s], wbf[li, :, o:o + 544], f"hw{s}", [wreg[(li, "att")]], [hwb[s]])

            def proj_gens(h):
                s = h % 2

                def g_q(tt):
                    def proj():
                        for kc in range(3):
                            mm(psP[0][0:96, :], HW[s][:, kc * 96:(kc + 1) * 96], Av(kc, tt * 512, 512), kc == 0, kc == 2,
                               [hwb[s], Ab[kc][tt]], [bP[0]], kc == 2)
                    return head_tile(psP[0], bP[0], 96, pcol("mla_q", j, 0, 96), CM[0:96, C_B96:C_B96 + 96],
                                     CM[0:96, C_R96:C_R96 + 96], tt * 512,
                                     [(MQ[s][0:96, tt * 512:(tt + 1) * 512], [mqb[s][tt]], 0, 96)], 0, proj)

                def g_k(tt):
                    def proj():
                        for kc in range(2):
                            mm(psP[1][0:64, :], HW[s][:, 288 + kc * 64: 288 + (kc + 1) * 64], Av(3 + kc, tt * 512, 512), kc == 0, kc == 1,
                               [hwb[s], Ab[3 + kc][tt]], [bP[1]], kc == 1)
                    return head_tile(psP[1], bP[1], 64, pcol("mla_kn", j, 0, 64), CM[0:64, C_B96:C_B96 + 64],
                                     None, tt * 512,
                                     [(MK[s][0:64, tt * 512:(tt + 1) * 512], [mkb[s][tt]], 0, 64)], 1, proj)

                def g_v(g8):
                    for i8 in range(8):
                        t16 = g8 * 8 + i8
                        for kc in range(2):
                            mm(psP[1][:, i8 * 64:(i8 + 1) * 64], Av(3 + kc, t16 * 128, 128), HW[s][:, 416 + kc * 64: 416 + (kc + 1) * 64],
                               kc == 0, kc == 1, [hwb[s], Ab[3 + kc][t16 // 4]], [bP[1]], (i8 == 7 and kc == 1))
                    yield
                    vcol = 0 if s == 0 else 64
                    base = MV0 + s * 2048 + g8 * 8 * 128 + vcol
                    dst = BU[:, base: base + 8 * 128].rearrange("p (a b) -> p a b", b=128)[:, :, 0:64]
                    src = psP[1][:, :].rearrange("p (a b) -> p a b", b=64)
                    copy_("dve", dst, src, [bP[1]], [mvb[s]])
                    yield

                lane_a = [g_q(tt) for tt in range(4)]
                lane_b = [g_k(tt) for tt in range(4)] + [g_v(0), g_v(1)]
                return [lane_a, lane_b]

            def unit(h):
                s = h % 2
                c = h // 2
                r0 = 0 if s == 0 else 64
                return dict(
                    q=(lambda qc: MQ[s][:, qc * 512:(qc + 1) * 512]),
                    qb=(lambda qc: [mqb[s][qc]]),
                    k=(lambda kt: MK[s][:, kt * 128:(kt + 1) * 128]),
                    kb=(lambda kt: [mkb[s][kt // 4]]),
                    v=(lambda kt: BU[:, MV0 + s * 2048 + kt * 128: MV0 + s * 2048 + (kt + 1) * 128]),
                    vb=[mvb[s]],
                    top=(s == 0),
                    out=(lambda qc: (Bv(c, qc * 512, 512, r0, r0 + 64), [Bb[c][qc]])),
                )

            if DBG_STOP < 2:
                return
            load_hw(0)
            load_hw(1)
            sp0 = Sprinkle(proj_gens(0))
            sp0.drain()
            for h in range(16):
                spr = Sprinkle(proj_gens(h + 1) if h + 1 < 16 else [], every=1)
                if h + 2 < 16:
                    load_hw(h + 2)
                attention([unit(h)], spr, 96 ** -0.5)
            flush_deferred()
            if DBG_STOP < 3:
                return
            wo_phase(li, MLA_WO, lambda c, t0, n: Bv(c, t0, n), Bb)

        P.barrier([(("d", "cm"), 16), (("d", "par"), 16)])
        for sq_i in range(n_seq):
            for kc in range(8):
                dma("sp", xT[:, kc * S:(kc + 1) * S], xin[sq_i, kc * 128:(kc + 1) * 128, :], f"xl{kc}", [], xb[kc])
            for li in layers:
                if li % 2 == 0:
                    mla(li)
                else:
                    gqa(li)
                P.barrier()
                last = (li == layers[-1])
                if DBG_STOP >= 4:
                    ffn(li, INCR_NORM and not last, sq_i if last else None)
                P.barrier()
            if not layers or DBG_STOP < 4:
                for kc in range(8):
                    dma("sp", yout[sq_i, kc * 128:(kc + 1) * 128, :], xT[:, kc * S:(kc + 1) * S], f"xs{kc}", xb[kc], [])
        P.wait_only("sp", [(("d", f"xs{kc}"), P.dcnt[("d", f"xs{kc}")]) for kc in range(8)])

        sems = {}
        for k in P.inckeys:
            sems[k] = es.enter_context(nc.semaphore("s_" + "_".join(str(t) for t in k)))
        with nc.Block() as block:
            def replay(e, ops):
                for (waits, fn, inc, amt) in ops:
                    for (k, v) in waits:
                        e.wait_ge(sems[k], v)
                    if fn is not None:
                        ins = fn(e)
                        if inc is not None:
                            ins.then_inc(sems[inc], amt)

            @block.sync
            def _(e):
                replay(e, P.ops["sp"])

            @block.tensor
            def _(e):
                replay(e, P.ops["pe"])

            @block.scalar
            def _(e):
                replay(e, P.ops["act"])

            @block.vector
            def _(e):
                replay(e, P.ops["dve"])

            @block.gpsimd
            def _(e):
                replay(e, P.ops["pool"])
    return nc, P


_CONSTS = None


def kernel(**inputs):
    global _CONSTS
    inp = {k: np.asarray(v) for k, v in inputs.items()}
    x = inp["x"].astype(np.float32, copy=False)
    wb, par = _host_layout(inp)
    if _CONSTS is None:
        _CONSTS = _host_consts()
    cm, tab = _CONSTS
    tab2 = np.ascontiguousarray(tab.reshape(2, 128, 2 * S))
    nseq = x.shape[0] // NCORE
    nc, _ = build_nc(nseq, (0, 1, 2, 3))
    in_maps = []
    for c in range(NCORE):
        xs = np.ascontiguousarray(x[c * nseq:(c + 1) * nseq].transpose(0, 2, 1))
        in_maps.append({"xT": xs, "wblob": wb, "params": par, "cmat": cm, "tabs": tab2})
    res = run_bass_kernel_spmd(nc, in_maps, core_ids=list(range(NCORE)))
    outs = [np.asarray(r["yT"]).transpose(0, 2, 1) for r in res.results]
    return np.ascontiguousarray(np.concatenate(outs, axis=0)).astype(np.float32, copy=False)
```

```python
import numpy as np
import ml_dtypes
from contextlib import ExitStack
import concourse.bass as bass
import concourse.mybir as mybir
from concourse.bass_utils import run_bass_kernel_spmd

F32 = mybir.dt.float32
BF16 = mybir.dt.bfloat16
ALU = mybir.AluOpType
AF = mybir.ActivationFunctionType

D = 1024
S = 2048
NCORE = 8
DFF = 2816
NJ = DFF // 128
EPS = 1e-6
DEPTH = 4
EH = [0, 1, 2, 3, 8, 9, 10, 11]
OH = [4, 5, 6, 7, 12, 13, 14, 15]

MLA_WIN = 0
MLA_WKR = MLA_WIN + 8 * 640
MLA_WH = MLA_WKR + 8 * 96
MLA_WO = MLA_WH + 16 * 544
MLA_END = MLA_WO + 8 * 1024
GQA_WQ = 0
GQA_WK = GQA_WQ + 8 * 1024
GQA_WV = GQA_WK + 2 * 1024
GQA_WO = GQA_WV + 8 * 256
GQA_END = GQA_WO + 8 * 1024
ATT_END = max(MLA_END, GQA_END)
FFN_GU = ATT_END
FFN_DN = FFN_GU + NJ * 2048
XL = FFN_DN + 8 * NJ * 128

def _pcols():
    off = 0
    cols = {}
    for j in range(2):
        for nm, n in (("mla_norm", 8), ("q_lora", 3), ("kv_lora", 2), ("mla_q", 1), ("mla_kn", 1), ("mla_kr", 1)):
            cols[(nm, j)] = off
            off += n
    for j in range(2):
        for nm, n in (("gqa_norm", 8), ("gqa_q", 1), ("gqa_k", 1)):
            cols[(nm, j)] = off
            off += n
    for i in range(4):
        cols[("ffn_norm", i)] = off
        off += 8
    return cols, off
PCOL, NPAR = _pcols()

C_ONES, C_B128, C_B96, C_R128, C_R96 = 0, 128, 256, 384, 512


def _host_consts():
    cm = np.zeros((128, 5 * 128), np.float32)
    cm[:, C_ONES:C_ONES + 128] = 1.0
    cm[0:64, C_B128:C_B128 + 64] = 1.0 / 64
    cm[64:128, C_B128 + 64:C_B128 + 128] = 1.0 / 64
    cm[0:64, C_B96:C_B96 + 64] = 1.0 / 64
    cm[64:96, C_B96 + 64:C_B96 + 96] = 1.0 / 32
    for hb in (0, 64):
        for blk in (0, 32):
            for i in range(32):
                dst = hb + blk + i
                if i < 16:
                    cm[dst + 16, C_R128 + dst] = -1.0
                else:
                    cm[dst - 16, C_R128 + dst] = 1.0
    for blk in (0, 16):
        for i in range(16):
            dst = 64 + blk + i
            if i < 8:
                cm[dst + 8, C_R96 + dst] = -1.0
            else:
                cm[dst - 8, C_R96 + dst] = 1.0
    t = np.arange(S)
    row = (t // 64).astype(np.float32)
    col = (t % 64).astype(np.float32)
    tab = np.zeros((2, 128, 2, S), np.float32)
    tab[0, :, 0, :] = 1.0
    inv8 = (np.float32(10000.0) ** (-np.arange(0, 16, 2, dtype=np.float32) / np.float32(16))).astype(np.float32)
    for blk, pos in ((0, row), (16, col)):
        for i in range(16):
            ang = (pos * inv8[i % 8]).astype(np.float32)
            tab[0, 64 + blk + i, 0, :] = np.cos(ang)
            tab[0, 64 + blk + i, 1, :] = np.sin(ang)
    inv16 = (np.float32(10000.0) ** (-np.arange(0, 32, 2, dtype=np.float32) / np.float32(32))).astype(np.float32)
    for hb in (0, 64):
        for blk, pos in ((0, row), (32, col)):
            for i in range(32):
                ang = (pos * inv16[i % 16]).astype(np.float32)
                tab[1, hb + blk + i, 0, :] = np.cos(ang)
                tab[1, hb + blk + i, 1, :] = np.sin(ang)
    return cm.astype(ml_dtypes.bfloat16), tab.astype(ml_dtypes.bfloat16)


def _host_layout(inp):
    wb = np.zeros((DEPTH, 128, XL), np.float32)
    par = np.zeros((128, NPAR), np.float32)

    def kc_slab(w, cols):
        k = w.shape[0] // 128
        return w[:, cols].reshape(k, 128, len(cols)).transpose(1, 0, 2).reshape(128, -1)

    for i in range(DEPTH):
        j = i // 2
        if i % 2 == 0:
            w_in = inp["mla_w_in"][j]
            wb[i, :, MLA_WIN:MLA_WIN + 8 * 640] = kc_slab(w_in, np.arange(640))
            kr = np.zeros((1024, 96), np.float32)
            kr[:, 64:96] = w_in[:, 640:672]
            wb[i, :, MLA_WKR:MLA_WKR + 8 * 96] = kc_slab(kr, np.arange(96))
            wuq = inp["mla_w_uq"][j]
            wukv = inp["mla_w_ukv"][j]
            for h in range(16):
                o = MLA_WH + h * 544
                wb[i, :, o:o + 288] = kc_slab(wuq, np.arange(h * 96, h * 96 + 96))
                wb[i, :, o + 288:o + 416] = kc_slab(wukv, np.arange(h * 128, h * 128 + 64))
                wb[i, :, o + 416:o + 544] = kc_slab(wukv, np.arange(h * 128 + 64, h * 128 + 128))
            wo = inp["mla_w_o"][j]
            for m in range(8):
                wb[i, :, MLA_WO + m * 1024:MLA_WO + (m + 1) * 1024] = kc_slab(wo, np.arange(m * 128, m * 128 + 128))
            par[:, PCOL[("mla_norm", j)]:PCOL[("mla_norm", j)] + 8] = inp["mla_norm"][j].reshape(8, 128).T
            par[:, PCOL[("q_lora", j)]:PCOL[("q_lora", j)] + 3] = inp["mla_q_lora_norm"][j].reshape(3, 128).T
            par[:, PCOL[("kv_lora", j)]:PCOL[("kv_lora", j)] + 2] = inp["mla_kv_lora_norm"][j].reshape(2, 128).T
            par[0:96, PCOL[("mla_q", j)]] = inp["mla_q_norm"][j]
            par[0:64, PCOL[("mla_kn", j)]] = inp["mla_k_norm"][j][:64]
            par[64:96, PCOL[("mla_kr", j)]] = inp["mla_k_norm"][j][64:96]
        else:
            wqkv = inp["gqa_w_qkv"][j]
            for c in range(8):
                cols = np.concatenate([np.arange(EH[c] * 64, EH[c] * 64 + 64), np.arange(OH[c] * 64, OH[c] * 64 + 64)])
                wb[i, :, GQA_WQ + c * 1024:GQA_WQ + (c + 1) * 1024] = kc_slab(wqkv, cols)
            for t in range(2):
                wb[i, :, GQA_WK + t * 1024:GQA_WK + (t + 1) * 1024] = kc_slab(wqkv, np.arange(1024 + t * 128, 1024 + t * 128 + 128))
            wb[i, :, GQA_WV:GQA_WV + 2048] = kc_slab(wqkv, np.arange(1280, 1536))
            wo = inp["gqa_w_o"][j]
            rows = np.concatenate([np.concatenate([np.arange(EH[c] * 64, EH[c] * 64 + 64), np.arange(OH[c] * 64, OH[c] * 64 + 64)]) for c in range(8)])
            wop = wo[rows, :]
            for m in range(8):
                wb[i, :, GQA_WO + m * 1024:GQA_WO + (m + 1) * 1024] = kc_slab(wop, np.arange(m * 128, m * 128 + 128))
            par[:, PCOL[("gqa_norm", j)]:PCOL[("gqa_norm", j)] + 8] = inp["gqa_norm"][j].reshape(8, 128).T
            par[:, PCOL[("gqa_q", j)]] = np.tile(inp["gqa_q_norm"][j], 2)
            par[:, PCOL[("gqa_k", j)]] = np.tile(inp["gqa_k_norm"][j], 2)
        wgu = inp["ffn_w_gate_up"][i]
        for jj in range(NJ):
            cols = np.concatenate([np.arange(jj * 128, jj * 128 + 128), np.arange(DFF + jj * 128, DFF + jj * 128 + 128)])
            wb[i, :, FFN_GU + jj * 2048:FFN_GU + (jj + 1) * 2048] = kc_slab(wgu, cols)
        wdn = inp["ffn_w_down"][i]
        for m in range(8):
            wb[i, :, FFN_DN + m * NJ * 128:FFN_DN + (m + 1) * NJ * 128] = kc_slab(wdn, np.arange(m * 128, m * 128 + 128))
        par[:, PCOL[("ffn_norm", i)]:PCOL[("ffn_norm", i)] + 8] = inp["ffn_norm"][i].reshape(8, 128).T
    return wb, par


class Buf:
    __slots__ = ("name", "w", "r", "excl")

    def __init__(self, name, excl=False):
        self.name = name
        self.w = None
        self.r = {}
        self.excl = excl


ENGS = ("pe", "act", "dve", "pool", "sp")
LIMIT = 16000


class Plan:
    def __init__(self):
        self.ops = {e: [] for e in ENGS}
        self.cnt = {e: 0 for e in ENGS}
        self.gen = {e: 0 for e in ENGS}
        self.waited = {e: {} for e in ENGS}
        self.dcnt = {}
        self.inckeys = []
        self._incset = set()

    def _addkey(self, k):
        if k not in self._incset:
            self._incset.add(k)
            self.inckeys.append(k)

    def _collect(self, reads, writes):
        deps = []
        for b in reads:
            if b.w is not None:
                deps.append(b.w)
        for b in writes:
            if b.w is not None:
                deps.append(b.w)
            for k, v in b.r.items():
                if k[0] == "e":
                    deps.append((("e", k[1], v[0]), v[1]))
                else:
                    deps.append((k, v))
        return deps

    def _need(self, eng, deps):
        out = {}
        wd = self.waited[eng]
        for key, val in deps:
            if key[0] == "e":
                if key[1] == "pe" and eng == "pe":
                    continue
                k2 = ("e", key[1])
                cur = wd.get(k2)
                if cur is not None and cur >= (key[2], val):
                    continue
                prev = out.get(k2)
                if prev is None or (key[2], val) > prev:
                    out[k2] = (key[2], val)
            else:
                if wd.get(key, 0) >= val:
                    continue
                if out.get(key, 0) < val:
                    out[key] = val
        waits = []
        for k, v in out.items():
            wd[k] = v
            if k[0] == "e":
                waits.append((("e", k[1], v[0]), v[1]))
            else:
                waits.append((k, v))
        return waits

    def _mark(self, tk, reads, writes):
        key, val = tk
        for b in reads:
            if key[0] == "e":
                k2 = ("e", key[1])
                nv = (key[2], val)
                if b.r.get(k2, (-1, -1)) < nv:
                    b.r[k2] = nv
            else:
                if b.r.get(key, 0) < val:
                    b.r[key] = val
        for b in writes:
            b.w = tk
            b.r = {}

    def op(self, eng, fn, reads=(), writes=(), signal=True):
        if any(b.excl for b in reads):
            writes = list(writes) + [b for b in reads if b.excl]
            reads = [b for b in reads if not b.excl]
        waits = self._need(eng, self._collect(reads, writes))
        if signal:
            if self.cnt[eng] >= LIMIT:
                self.gen[eng] += 1
                self.cnt[eng] = 0
            self.cnt[eng] += 1
            key = ("e", eng, self.gen[eng])
            tk = (key, self.cnt[eng])
            self._addkey(key)
            self.ops[eng].append((waits, fn, key, 1))
        else:
            g, c = self.gen[eng], self.cnt[eng]
            if c >= LIMIT:
                g += 1
                c = 0
            tk = (("e", eng, g), c + 1)
            self.ops[eng].append((waits, fn, None, 0))
        self._mark(tk, reads, writes)
        return tk

    def dma(self, q, fn, semid, reads=(), writes=()):
        waits = self._need(q, self._collect(reads, writes))
        key = ("d", semid)
        self.dcnt[key] = self.dcnt.get(key, 0) + 16
        tk = (key, self.dcnt[key])
        self._addkey(key)
        self.ops[q].append((waits, fn, key, 16))
        self._mark(tk, reads, writes)
        return tk

    def last_tickets(self):
        tks = []
        for e in ("pe", "act", "dve", "pool"):
            if self.gen[e] > 0 or self.cnt[e] > 0:
                tks.append((("e", e, self.gen[e]), self.cnt[e]))
        return tks

    def barrier(self, extra=()):
        tks = self.last_tickets() + list(extra)
        for e in ENGS:
            waits = self._need(e, tks)
            if waits:
                self.ops[e].append((waits, None, None, 0))

    def wait_only(self, eng, tks):
        waits = self._need(eng, tks)
        if waits:
            self.ops[eng].append((waits, None, None, 0))


class Sprinkle:
    def __init__(self, lanes, every=1):
        self.lanes = [list(l) for l in lanes]
        self.every = every
        self.i = 0

    def tick(self):
        self.i += 1
        if self.i % self.every == 0:
            self.step()

    def step(self):
        for l in self.lanes:
            while l:
                try:
                    next(l[0])
                    break
                except StopIteration:
                    l.pop(0)

    def drain(self):
        while any(self.lanes):
            self.step()


DBG_STOP = 99
SQ_POOL = True
INCR_NORM = True
POOL = "dve"


def build_nc(n_seq=4, layers=(0, 1, 2, 3), dbg=False):
    nc = bass.Bass("TRN2", target_bir_lowering=False)
    xin = nc.dram_tensor("xT", [n_seq, D, S], F32, kind="ExternalInput").ap()
    wfp = nc.dram_tensor("wblob", [DEPTH, 128, XL], F32, kind="ExternalInput").ap()
    par_d = nc.dram_tensor("params", [128, NPAR], F32, kind="ExternalInput").ap()
    cm_d = nc.dram_tensor("cmat", [128, 640], BF16, kind="ExternalInput").ap()
    tab_d = nc.dram_tensor("tabs", [2, 128, 2 * S], BF16, kind="ExternalInput").ap()
    wbf = nc.dram_tensor("wbf", [DEPTH, 128, XL], BF16, kind="Internal").ap()
    yout = nc.dram_tensor("yT", [n_seq, D, S], F32, kind="ExternalOutput").ap()

    P = Plan()
    with ExitStack() as es:
        def sb(name, shape, dt):
            return es.enter_context(nc.sbuf_tensor(name, shape, dt))

        def ps(name, shape):
            return es.enter_context(nc.psum_tensor(name, shape, F32))

        xT = sb("xT_sb", [128, 8 * S], F32)
        A = sb("A_sb", [128, 8 * S], BF16)
        BU = sb("BU_sb", [128, 32768], BF16)
        T32 = sb("T32_sb", [128, 4096], F32)
        TAB = sb("TAB_sb", [128, 2 * S], BF16)
        WT = sb("WT_sb", [128, 9280], BF16)
        CM = sb("CM_sb", [128, 640], BF16)
        PAR = sb("PAR_sb", [128, NPAR], F32)
        CMG = sb("CMG_sb", [128, 256], BF16)
        GT = sb("GT_sb", [128, 4], F32)
        cmgb = Buf("cmg")
        gtb = Buf("gt")
        psS = [ps("psS0", [128, 1024]), ps("psS1", [128, 1024])]
        psO = [ps("psO0", [128, 512]), ps("psO1", [128, 512])]
        psP = [ps("psP1", [128, 512]), ps("psP2", [128, 512])]

        xb = [[Buf(f"x{kc}_{tt}") for tt in range(4)] for kc in range(8)]
        Ab = [[Buf(f"A{kc}_{tt}") for tt in range(4)] for kc in range(8)]
        Bb = [[Buf(f"B{kc}_{tt}") for tt in range(4)] for kc in range(8)]
        bS = [[Buf("S0a", True), Buf("S0b", True)], [Buf("S1a", True), Buf("S1b", True)]]
        bO = [Buf("O0", True), Buf("O1", True)]
        bP = [Buf("P1", True), Buf("P2", True)]
        tabb = Buf("tab")
        cmb = Buf("cm")
        parb = Buf("par")
        sqb = [Buf("sq0"), Buf("sq1")]
        qgb = [Buf("qg0"), Buf("qg1")]
        rstdb = [Buf("rstd0"), Buf("rstd1")]
        ab = [Buf("a0"), Buf("a1")]
        bb = [Buf("b0"), Buf("b1")]
        rdenb = [Buf("rden0"), Buf("rden1")]
        ptb = [Buf("pt0"), Buf("pt1")]
        wreg = {}

        def xv(kc, tt):
            return xT[:, kc * S + tt * 512: kc * S + (tt + 1) * 512]

        def Av(kc, t0, n, r0=0, r1=128):
            return A[r0:r1, kc * S + t0: kc * S + t0 + n]

        def Bv(kc, t0, n, r0=0, r1=128):
            return BU[r0:r1, kc * S + t0: kc * S + t0 + n]

        U0 = 16384
        PT = [BU[:, U0 + 12288 + s * 1024: U0 + 12288 + (s + 1) * 1024] for s in range(2)]
        SQ = [BU[:, U0 + 14336 + s * 512: U0 + 14336 + (s + 1) * 512] for s in range(2)]
        QG = [BU[:, U0 + 15360 + s * 512: U0 + 15360 + (s + 1) * 512] for s in range(2)]
        RSTD = [T32[:, s * 512:(s + 1) * 512] for s in range(2)]
        TA = [T32[:, 1024 + s * 512: 1024 + (s + 1) * 512] for s in range(2)]
        TB = [T32[:, 2048 + s * 512: 2048 + (s + 1) * 512] for s in range(2)]
        RDEN = [T32[:, 3072 + s * 512: 3072 + (s + 1) * 512] for s in range(2)]
        ones_ap = CM[:, C_ONES:C_ONES + 128]

        def cosv(t0, n, R):
            return TAB[0:R, t0:t0 + n]

        def sinv(t0, n, R):
            return TAB[0:R, S + t0:S + t0 + n]

        def pcol(name, idx, k=0, R=128, r0=0):
            c = PCOL[(name, idx)] + k
            return PAR[r0:R, c:c + 1]

        def mm(out, lhsT, rhs, start, stop, reads, writes, signal):
            P.op("pe", lambda e: e.matmul(out, lhsT=lhsT, rhs=rhs, start=start, stop=stop), reads, writes, signal)

        def act(out, in_, func, reads, writes, scale=1.0, bias=0.0):
            P.op("act", lambda e: e.activation(out=out, in_=in_, func=func, bias=bias, scale=scale), reads, writes)

        def tt_(eng, out, in0, in1, op, reads, writes):
            P.op(eng, lambda e: e.tensor_tensor(out=out, in0=in0, in1=in1, op=op), reads, writes)

        def ts_(eng, out, in0, scalar, op, reads, writes):
            P.op(eng, lambda e: e.tensor_scalar(out=out, in0=in0, scalar1=scalar, scalar2=None, op0=op), reads, writes)

        def stt_(eng, out, in0, scalar, in1, reads, writes):
            P.op(eng, lambda e: e.scalar_tensor_tensor(out=out, in0=in0, scalar=scalar, in1=in1, op0=ALU.mult, op1=ALU.mult), reads, writes)

        def copy_(eng, out, in_, reads, writes):
            P.op(eng, lambda e: e.tensor_copy(out=out, in_=in_), reads, writes)

        def dma(q, out, in_, semid, reads, writes):
            P.dma(q, lambda e: e.dma_start(out=out, in_=in_), semid, reads, writes)

        dma("sp", CM[:], cm_d, "cm", [], [cmb])
        dma("sp", PAR[:], par_d, "par", [], [parb])
        CH = 4096
        for li in layers:
            for (rname, c0, c1) in (("att", 0, ATT_END), ("gu", FFN_GU, FFN_DN), ("dn", FFN_DN, XL)):
                rb = Buf(f"w{li}{rname}")
                wreg[(li, rname)] = rb
                c = c0
                while c < c1:
                    n = min(CH, c1 - c)
                    dma("pool", wbf[li, :, c:c + n], wfp[li, :, c:c + n], f"cast{li}{rname}", [], [])
                    c += n
                key = ("d", f"cast{li}{rname}")
                rb.w = (key, P.dcnt[key])

        def rmsnorm_main(gname, gidx):
            for tt in range(4):
                for kc in range(8):
                    s = kc % 2
                    act(SQ[s], xv(kc, tt), AF.Square, [xb[kc][tt]], [sqb[s]])
                    mm(psP[1][:, :], ones_ap, SQ[s], kc == 0, kc == 7, [sqb[s], cmb], [bP[1]], True)
                act(RSTD[0], psP[1][:, :], AF.Ln, [bP[1]], [rstdb[0]], scale=1.0 / D, bias=EPS)
                act(RSTD[0], RSTD[0], AF.Exp, [rstdb[0]], [rstdb[0]], scale=-0.5)
                for kc in range(8):
                    stt_("dve", Av(kc, tt * 512, 512), xv(kc, tt), pcol(gname, gidx, kc), RSTD[0],
                         [xb[kc][tt], rstdb[0], parb], [Ab[kc][tt]])

        T32b = T32.bitcast(BF16)
        NSQ = [T32b[:, 6144 + s_ * 512: 6144 + (s_ + 1) * 512] for s_ in range(4)]
        nsqb = [Buf(f"nsq{s_}") for s_ in range(4)]
        NRSTD = [RSTD[0], RSTD[1], TB[0], TB[1]]
        nrstdb = [rstdb[0], rstdb[1], bb[0], bb[1]]
        nsq_i = [0]
        have_rstd = [False]

        def norm_stat(m, tt, bank_ap, bank_buf):
            s_ = nsq_i[0] % 4
            nsq_i[0] += 1
            act(NSQ[s_], xv(m, tt), AF.Square, [xb[m][tt]], [nsqb[s_], rdenb[s_ // 2]])
            mm(bank_ap, ones_ap, NSQ[s_], m == 0, m == 7, [nsqb[s_], rdenb[s_ // 2], cmb], [bank_buf], True)

        def norm_finish(tt, bank_ap, bank_buf):
            act(NRSTD[tt], bank_ap, AF.Ln, [bank_buf], [nrstdb[tt]], scale=1.0 / D, bias=EPS)
            act(NRSTD[tt], NRSTD[tt], AF.Exp, [nrstdb[tt]], [nrstdb[tt]], scale=-0.5)

        def norm_apply_one(gname, gidx, tt, kc):
            stt_("dve", Av(kc, tt * 512, 512), xv(kc, tt), pcol(gname, gidx, kc), NRSTD[tt],
                 [xb[kc][tt], nrstdb[tt], parb], [Ab[kc][tt]])

        def do_norm(gname, gidx, defer_from=4):
            later = []
            if have_rstd[0]:
                for tt in range(4):
                    for kc in range(8):
                        if tt < defer_from:
                            norm_apply_one(gname, gidx, tt, kc)
                        else:
                            later.append(lambda tt=tt, kc=kc: norm_apply_one(gname, gidx, tt, kc))
                have_rstd[0] = False
            else:
                rmsnorm_main(gname, gidx)
            return later

        def head_tile(pbank, pbuf, R, gain_ap, bones_ap, rot_ap, t0, outs, slot, proj_fn, qg_act=False, sq_pool=False):
            proj_fn()
            yield
            ps_ap = pbank[0:R, :]
            if sq_pool:
                ts_("dve", QG[slot][0:R, :], ps_ap, gain_ap, ALU.mult, [pbuf, parb], [qgb[slot]])
                yield
                tt_("pool", SQ[slot][0:R, :], QG[slot][0:R, :], QG[slot][0:R, :], ALU.mult, [qgb[slot]], [sqb[slot]])
                yield
            else:
                act(SQ[slot][0:R, :], ps_ap, AF.Square, [pbuf], [sqb[slot]])
                yield
            if sq_pool:
                pass
            elif qg_act:
                P.op("act", lambda e: e.activation(out=QG[slot][0:R, :], in_=ps_ap, func=AF.Identity, bias=0.0, scale=gain_ap),
                     [pbuf, parb], [qgb[slot]])
            else:
                ts_("dve", QG[slot][0:R, :], ps_ap, gain_ap, ALU.mult, [pbuf, parb], [qgb[slot]])
            if not sq_pool:
                yield
            mm(ps_ap, bones_ap, SQ[slot][0:R, :], True, True, [sqb[slot], cmb, cmgb], [pbuf], True)
            yield
            act(RSTD[slot][0:R, :], ps_ap, AF.Ln, [pbuf], [rstdb[slot]], scale=1.0, bias=EPS)
            act(RSTD[slot][0:R, :], RSTD[slot][0:R, :], AF.Exp, [rstdb[slot]], [rstdb[slot]], scale=-0.5)
            yield
            if rot_ap is not None:
                mm(ps_ap, rot_ap, QG[slot][0:R, :], True, True, [qgb[slot], cmb], [pbuf], True)
                tt_("dve", TA[slot][0:R, :], QG[slot][0:R, :], cosv(t0, 512, R), ALU.mult, [qgb[slot], tabb], [ab[slot]])
                yield
                tt_("dve", TB[slot][0:R, :], ps_ap, sinv(t0, 512, R), ALU.mult, [pbuf, tabb], [bb[slot]])
                tt_("dve", TA[slot][0:R, :], TA[slot][0:R, :], TB[slot][0:R, :], ALU.add, [ab[slot], bb[slot]], [ab[slot]])
                for (oap, obufs, r0, r1) in outs:
                    tt_("dve", oap, TA[slot][r0:r1, :], RSTD[slot][r0:r1, :], ALU.mult, [ab[slot], rstdb[slot]], obufs)
            else:
                for (oap, obufs, r0, r1) in outs:
                    tt_("dve", oap, QG[slot][r0:r1, :], RSTD[slot][r0:r1, :], ALU.mult, [qgb[slot], rstdb[slot]], obufs)
            yield

        def run_lanes(lanes):
            lanes = [list(l) for l in lanes]
            while any(lanes):
                for l in lanes:
                    while l:
                        try:
                            next(l[0])
                            break
                        except StopIteration:
                            l.pop(0)

        def run(gen):
            for _ in gen:
                pass

        deferred = []

        def push_norm(obank, obuf, rden, rdbuf, o0, oap, obufs):
            d0 = 64 - o0
            for c0 in range(0, 512, 128):
                deferred.append(lambda c0=c0: P.op(
                    "dve", (lambda e: e.reciprocal(out=rden[o0:o0 + 64, c0:c0 + 128], in_=obank[d0:d0 + 64, c0:c0 + 128])),
                    [obuf], [rdbuf]))
            deferred.append(lambda: tt_("dve", oap, obank[o0:o0 + 64, :], rden[o0:o0 + 64, :], ALU.mult, [obuf, rdbuf], obufs))

        def pop_deferred():
            if deferred:
                deferred.pop(0)()

        def flush_deferred():
            while deferred:
                deferred.pop(0)()

        def attention(units, sprinkle, scale):
            iters = [(u, qc, kp) for u in units for qc in range(4) for kp in range(8)]

            def qk(i):
                u, qc, kp = iters[i]
                s = i % 2
                for hh in range(2):
                    kt = 2 * kp + hh
                    mm(psS[s][:, hh * 512:(hh + 1) * 512], u["k"](kt), u["q"](qc), True, True,
                       u["kb"](kt) + u["qb"](qc), [bS[s][hh]], hh == 1)

            qk(0)
            for i, (u, qc, kp) in enumerate(iters):
                s = i % 2
                if i + 1 < len(iters):
                    qk(i + 1)
                act(PT[s], psS[s][:, :], AF.Exp, [bS[s][0], bS[s][1]], [ptb[s]], scale=scale)
                n = i // 8
                ob = n % 2
                for hh in range(2):
                    kt = 2 * kp + hh
                    mm(psO[ob][:, :], u["v"](kt), PT[s][:, hh * 512:(hh + 1) * 512],
                       kp == 0 and hh == 0, kp == 7 and hh == 1, [ptb[s]] + u["vb"], [bO[ob]], hh == 1)
                if kp == 7:
                    o0 = 0 if u["top"] else 64
                    oap, obufs = u["out"](qc)
                    push_norm(psO[ob], bO[ob], RDEN[ob], rdenb[ob], o0, oap, obufs)
                pop_deferred()
                sprinkle.tick()
            sprinkle.drain()

        WO_SLOT = [WT[:, s * 1024:(s + 1) * 1024] for s in range(2)]
        wob = [Buf("wo0"), Buf("wo1")]

        def wo_phase(li, wo_off, src_v, src_b):
            pend = []
            for m in range(8):
                s = m % 2
                dma("sp", WO_SLOT[s], wbf[li, :, wo_off + m * 1024: wo_off + (m + 1) * 1024], f"wo{s}",
                    [wreg[(li, "att")]], [wob[s]])
                for tt in range(4):
                    ob = tt % 2
                    for c in range(8):
                        mm(psO[ob][:, :], WO_SLOT[s][:, c * 128:(c + 1) * 128], src_v(c, tt * 512, 512), c == 0, c == 7,
                           [wob[s], src_b[c][tt]], [bO[ob]], c == 7)
                    tt_("dve", xv(m, tt), xv(m, tt), psO[ob][:, :], ALU.add, [xb[m][tt], bO[ob]], [xb[m][tt]])
                    if INCR_NORM:
                        pend.append((m, tt, psS[tt // 2][:, (tt % 2) * 512:(tt % 2 + 1) * 512], bS[tt // 2][tt % 2]))
                        if len(pend) > 2:
                            norm_stat(*pend.pop(0))
            if INCR_NORM:
                while pend:
                    norm_stat(*pend.pop(0))
                for tt in range(4):
                    norm_finish(tt, psS[tt // 2][:, (tt % 2) * 512:(tt % 2 + 1) * 512], bS[tt // 2][tt % 2])
                have_rstd[0] = True

        ACTT = lambda j, t0, n: BU[:, j * 1024 + t0: j * 1024 + t0 + n]
        WGU = [BU[:, 22528 + s * 2048: 22528 + (s + 1) * 2048] for s in range(2)]
        WDN = [BU[:, 26624 + s * 2816: 26624 + (s + 1) * 2816] for s in range(2)]
        wgub = [Buf("wgu0"), Buf("wgu1")]
        wdnb = [Buf("wdn0"), Buf("wdn1")]
        actb = [[Buf(f"act{j}_{t}") for t in range(2)] for j in range(NJ)]

        def ffn(li, stats, store_seq):
            pend = []
            later = do_norm("ffn_norm", li, defer_from=2)
            for half in range(2):
                for j in range(NJ):
                    if later:
                        later.pop(0)()
                    s = j % 2
                    dma("sp", WGU[s], wbf[li, :, FFN_GU + j * 2048: FFN_GU + (j + 1) * 2048], f"wgu{s}",
                        [wreg[(li, "gu")]], [wgub[s]])
                    for t2 in range(2):
                        tt = half * 2 + t2
                        for kc in range(8):
                            mm(psS[0][:, t2 * 512:(t2 + 1) * 512], WGU[s][:, kc * 256: kc * 256 + 128], Av(kc, tt * 512, 512),
                               kc == 0, kc == 7, [wgub[s], Ab[kc][tt]], [bS[0][t2]], kc == 7)
                        for kc in range(8):
                            mm(psS[1][:, t2 * 512:(t2 + 1) * 512], WGU[s][:, kc * 256 + 128: kc * 256 + 256], Av(kc, tt * 512, 512),
                               kc == 0, kc == 7, [wgub[s], Ab[kc][tt]], [bS[1][t2]], kc == 7)
                        act(TA[t2], psS[0][:, t2 * 512:(t2 + 1) * 512], AF.Silu, [bS[0][t2]], [ab[t2]])
                        tt_("dve", ACTT(j, t2 * 512, 512), TA[t2], psS[1][:, t2 * 512:(t2 + 1) * 512], ALU.mult,
                            [ab[t2], bS[1][t2]], [actb[j][t2]])
                while later:
                    later.pop(0)()
                for m in range(8):
                    s = m % 2
                    dma("sp", WDN[s], wbf[li, :, FFN_DN + m * 2816: FFN_DN + (m + 1) * 2816], f"wdn{s}",
                        [wreg[(li, "dn")]], [wdnb[s]])
                    for t2 in range(2):
                        tt = half * 2 + t2
                        for j in range(NJ):
                            mm(psO[t2][:, :], WDN[s][:, j * 128:(j + 1) * 128], ACTT(j, t2 * 512, 512), j == 0, j == NJ - 1,
                               [wdnb[s], actb[j][t2]], [bO[t2]], j == NJ - 1)
                        tt_("dve", xv(m, tt), xv(m, tt), psO[t2][:, :], ALU.add, [xb[m][tt], bO[t2]], [xb[m][tt]])
                        if stats:
                            pend.append((m, tt, psP[t2][:, :], bP[t2]))
                            if len(pend) > 2:
                                norm_stat(*pend.pop(0))
                    if store_seq is not None and half == 1:
                        dma("act", yout[store_seq, m * 128:(m + 1) * 128, :], xT[:, m * S:(m + 1) * S], f"xs{m}", xb[m], [])
                if stats:
                    while pend:
                        norm_stat(*pend.pop(0))
                    for t2 in range(2):
                        norm_finish(half * 2 + t2, psP[t2][:, :], bP[t2])
            if stats:
                have_rstd[0] = True

        GK = [BU[:, U0 + i * 2048: U0 + (i + 1) * 2048] for i in range(2)]
        gkb = [[Buf(f"gk{i}_{tt}") for tt in range(4)] for i in range(2)]
        GV0 = U0 + 4096
        gvb = [Buf(f"gv{j}") for j in range(4)]
        GW = [WT[:, 2048 + s * 1024: 2048 + (s + 1) * 1024] for s in range(3)]
        gwb = [Buf(f"gw{s}") for s in range(3)]
        GWV = WT[:, 5120:5120 + 2048]
        gwvb = Buf("gwv")

        def gqa(li):
            j = li // 2
            dma("sp", TAB[:], tab_d[1], "tab", [], [tabb])
            do_norm("gqa_norm", j)
            P.op(POOL, lambda e: e.memset(BU[:, GV0:GV0 + 8192], 1.0), [], gvb)
            wcount = [0]

            def load_w(off):
                s = wcount[0] % 3
                wcount[0] += 1
                dma("sp", GW[s], wbf[li, :, off:off + 1024], f"gw{s}", [wreg[(li, "att")]], [gwb[s]])
                return s

            lane_banks = [(psP[0], bP[0]), (psP[1], bP[1])]
            ws_map = {}

            def g_loadw(off, key):
                ws_map[key] = load_w(off)
                return
                yield

            def proj_gen(lane, key, tt, gain_ap, outs):
                pbank, pbuf = lane_banks[lane]

                def proj():
                    ws = ws_map[key]
                    for kc in range(8):
                        mm(pbank[:, :], GW[ws][:, kc * 128:(kc + 1) * 128], Av(kc, tt * 512, 512), kc == 0, kc == 7,
                           [gwb[ws], Ab[kc][tt]], [pbuf], kc == 7)
                return head_tile(pbank, pbuf, 128, gain_ap, CM[:, C_B128:C_B128 + 128], CM[:, C_R128:C_R128 + 128],
                                 tt * 512, outs, lane, proj, qg_act=True)

            klanes = [[g_loadw(GQA_WK, ("k", 0)), g_loadw(GQA_WK + 1024, ("k", 1))], []]
            for t in range(2):
                for tt in range(4):
                    lane = tt % 2
                    klanes[lane].append(proj_gen(lane, ("k", t), tt, pcol("gqa_k", j),
                                                 [(GK[t][:, tt * 512:(tt + 1) * 512], [gkb[t][tt]], 0, 128)]))
            run_lanes(klanes)
            dma("sp", GWV, wbf[li, :, GQA_WV:GQA_WV + 2048], "gwv", [wreg[(li, "att")]], [gwvb])
            for t16 in range(16):
                bank = psS[(t16 // 2) % 2]
                hb = t16 % 2
                bbuf = bS[(t16 // 2) % 2][hb]
                for kc in range(8):
                    mm(bank[:, hb * 512: hb * 512 + 256], Av(kc, t16 * 128, 128), GWV[:, kc * 256:(kc + 1) * 256], kc == 0, kc == 7,
                       [gwvb, Ab[kc][t16 // 4]], [bbuf], kc == 7)
                for kv in range(4):
                    dst = GV0 + kv * 2048 + t16 * 128 + (kv % 2) * 64
                    copy_("dve", BU[:, dst:dst + 64], bank[:, hb * 512 + kv * 64: hb * 512 + kv * 64 + 64], [bbuf], [gvb[kv]])
            qlanes = [[g_loadw(GQA_WQ, ("q", 0))], []]
            for c in range(8):
                if c + 1 < 8:
                    qlanes[0].append(g_loadw(GQA_WQ + (c + 1) * 1024, ("q", c + 1)))
                for tt in range(4):
                    lane = tt % 2
                    qlanes[lane].append(proj_gen(lane, ("q", c), tt, pcol("gqa_q", j),
                                                 [(Bv(c, tt * 512, 512), [Bb[c][tt]], 0, 128)]))
            run_lanes(qlanes)
            iters = [(c, qc, kt) for c in range(8) for qc in range(4) for kt in range(16)]
            obanks = [(psO[0], bO[0]), (psO[1], bO[1]), (psP[0], bP[0]), (psP[1], bP[1])]

            def qk(i):
                c, qc, kt = iters[i]
                s = i % 2
                t = 0 if c < 4 else 1
                for hh in range(2):
                    r0 = hh * 64
                    mm(psS[s][:, hh * 512:(hh + 1) * 512], GK[t][r0:r0 + 64, kt * 128:(kt + 1) * 128],
                       Bv(c, qc * 512, 512, r0, r0 + 64), True, True, [gkb[t][kt // 4], Bb[c][qc]], [bS[s][hh]], hh == 1)

            qk(0)
            for i, (c, qc, kt) in enumerate(iters):
                s = i % 2
                if i + 1 < len(iters):
                    qk(i + 1)
                act(PT[s], psS[s][:, :], AF.Exp, [bS[s][0], bS[s][1]], [ptb[s]], scale=64 ** -0.5)
                u = (i // 16) % 2
                for hh in range(2):
                    kv = (EH[c] if hh == 0 else OH[c]) // 4
                    ob, obuf = obanks[2 * u + hh]
                    vo = GV0 + kv * 2048 + kt * 128
                    mm(ob[:, :], BU[:, vo:vo + 128], PT[s][:, hh * 512:(hh + 1) * 512], kt == 0, kt == 15,
                       [ptb[s], gvb[kv]], [obuf], hh == 1)
                if kt == 15:
                    for hh in range(2):
                        ob, obuf = obanks[2 * u + hh]
                        push_norm(ob, obuf, RDEN[hh], rdenb[hh], hh * 64, Av(c, qc * 512, 512, hh * 64, hh * 64 + 64), [Ab[c][qc]])
                pop_deferred()
            flush_deferred()
            wo_phase(li, GQA_WO, lambda c, t0, n: Av(c, t0, n), Ab)

        MQ = [BU[:, U0 + s * 2048: U0 + (s + 1) * 2048] for s in range(2)]
        MK = [BU[:, U0 + 4096 + s * 2048: U0 + 4096 + (s + 1) * 2048] for s in range(2)]
        MV0 = U0 + 8192
        mqb = [[Buf(f"mq{s}_{tt}") for tt in range(4)] for s in range(2)]
        mkb = [[Buf(f"mk{s}_{tt}") for tt in range(4)] for s in range(2)]
        mvb = [Buf("mv0"), Buf("mv1")]
        WIN = WT[:, 2048:2048 + 5120]
        WKR = WT[:, 7168:7168 + 768]
        winb = Buf("win")
        HW = [WT[:, 7936 + s * 544: 7936 + (s + 1) * 544] for s in range(2)]
        hwb = [Buf("hw0"), Buf("hw1")]

        def mla(li):
            j = li // 2
            if DBG_STOP < 1:
                return
            dma("sp", TAB[:], tab_d[0], "tab", [], [tabb])
            dma("sp", WT[:, 2048:2048 + 5888], wbf[li, :, MLA_WIN:MLA_WIN + 5888], "win", [wreg[(li, "att")]], [winb])
            mla_later = do_norm("mla_norm", j)
            P.op(POOL, lambda e: e.memset(BU[:, MV0:MV0 + 4096], 1.0), [], mvb)
            for s_ in range(2):
                P.op("dve", (lambda e, s_=s_: e.memset(MQ[s_][64:128, :], 0.0)), [], mqb[s_])
                P.op("dve", (lambda e, s_=s_: e.memset(MK[s_][64:128, :], 0.0)), [], mkb[s_])
            banks = [(psS[0][:, 0:512], bS[0][0]), (psS[0][:, 512:1024], bS[0][1]), (psS[1][:, 0:512], bS[1][0]),
                     (psS[1][:, 512:1024], bS[1][1]), (psO[0][:, :], bO[0])]
            for tt in range(4):
                for _ in range(8):
                    if mla_later:
                        mla_later.pop(0)()
                hT_b = [Ab[kc][tt] for kc in range(8)]
                for ch in range(5):
                    bk, bkb = banks[ch]
                    for kc in range(8):
                        mm(bk, WIN[:, kc * 640 + ch * 128: kc * 640 + (ch + 1) * 128], Av(kc, tt * 512, 512), kc == 0, kc == 7,
                           [winb, Ab[kc][tt]], [bkb], kc == 7)
                    s = ch % 2
                    act(SQ[s], bk, AF.Square, [bkb], [sqb[s]])
                    if ch < 3:
                        mm(psP[0][:, :], ones_ap, SQ[s], ch == 0, ch == 2, [sqb[s], cmb], [bP[0]], True)
                    else:
                        mm(psP[1][:, :], ones_ap, SQ[s], ch == 3, ch == 4, [sqb[s], cmb], [bP[1]], True)
                for kc in range(8):
                    mm(psO[1][0:96, :], WKR[:, kc * 96:(kc + 1) * 96], Av(kc, tt * 512, 512), kc == 0, kc == 7,
                       [winb, Ab[kc][tt]], [bO[1]], kc == 7)
                act(RSTD[0], psP[0][:, :], AF.Ln, [bP[0]], [rstdb[0]], scale=1.0 / 384, bias=EPS)
                act(RSTD[0], RSTD[0], AF.Exp, [rstdb[0]], [rstdb[0]], scale=-0.5)
                act(RSTD[1], psP[1][:, :], AF.Ln, [bP[1]], [rstdb[1]], scale=1.0 / 256, bias=EPS)
                act(RSTD[1], RSTD[1], AF.Exp, [rstdb[1]], [rstdb[1]], scale=-0.5)
                for ch in range(5):
                    bk, bkb = banks[ch]
                    if ch < 3:
                        g_ap, r = pcol("q_lora", j, ch), 0
                    else:
                        g_ap, r = pcol("kv_lora", j, ch - 3), 1
                    stt_("dve", Av(ch, tt * 512, 512), bk, g_ap, RSTD[r], [bkb, rstdb[r], parb] , [Ab[ch][tt]] + ([] if ch else hT_b[5:]))
                run(head_tile(psO[1], bO[1], 96, pcol("mla_kr", j, 0, 96), CM[0:96, C_B96:C_B96 + 96],
                              CM[0:96, C_R96:C_R96 + 96], tt * 512,
                              [(MK[0][64:96, tt * 512:(tt + 1) * 512], [mkb[0][tt]], 64, 96),
                               (MK[1][64:96, tt * 512:(tt + 1) * 512], [mkb[1][tt]], 64, 96)], 0, lambda: None))

            if SQ_POOL:
                for (gn, R_, c0) in (("mla_q", 96, 0), ("mla_kn", 64, 128)):
                    gcol_ap = pcol(gn, j, 0, R_)
                    gi = GT[0:R_, (0 if c0 == 0 else 1):(1 if c0 == 0 else 2)]
                    P.op("dve", (lambda e, gi=gi, gcol_ap=gcol_ap: e.reciprocal(out=gi, in_=gcol_ap)), [parb], [gtb])
                    tt_("dve", gi, gi, gi, ALU.mult, [gtb], [gtb])
                    ts_("dve", CMG[0:R_, c0:c0 + R_], CM[0:R_, C_B96:C_B96 + R_], gi, ALU.mult, [cmb, gtb], [cmgb])

            def load_hw(h):
                s = h % 2
                o = MLA_WH + h * 544
                dma("sp", HW[s], wbf[li, :, o:o + 544], f"hw{s}", [wreg[(li, "att")]], [hwb[s]])

            def proj_gens(h):
                s = h % 2

                def g_q(tt):
                    def proj():
                        for kc in range(3):
                            mm(psP[0][0:96, :], HW[s][:, kc * 96:(kc + 1) * 96], Av(kc, tt * 512, 512), kc == 0, kc == 2,
                               [hwb[s], Ab[kc][tt]], [bP[0]], kc == 2)
                    return head_tile(psP[0], bP[0], 96, pcol("mla_q", j, 0, 96),
                                     CMG[0:96, 0:96] if SQ_POOL else CM[0:96, C_B96:C_B96 + 96],
                                     CM[0:96, C_R96:C_R96 + 96], tt * 512,
                                     [(MQ[s][0:96, tt * 512:(tt + 1) * 512], [mqb[s][tt]], 0, 96)], 0, proj, sq_pool=SQ_POOL)

                def g_k(tt):
                    def proj():
                        for kc in range(2):
                            mm(psP[1][0:64, :], HW[s][:, 288 + kc * 64: 288 + (kc + 1) * 64], Av(3 + kc, tt * 512, 512), kc == 0, kc == 1,
                               [hwb[s], Ab[3 + kc][tt]], [bP[1]], kc == 1)
                    return head_tile(psP[1], bP[1], 64, pcol("mla_kn", j, 0, 64),
                                     CMG[0:64, 128:192] if SQ_POOL else CM[0:64, C_B96:C_B96 + 64],
                                     None, tt * 512,
                                     [(MK[s][0:64, tt * 512:(tt + 1) * 512], [mkb[s][tt]], 0, 64)], 1, proj, sq_pool=SQ_POOL)

                def g_v(g8):
                    for i8 in range(8):
                        t16 = g8 * 8 + i8
                        for kc in range(2):
                            mm(psP[1][:, i8 * 64:(i8 + 1) * 64], Av(3 + kc, t16 * 128, 128), HW[s][:, 416 + kc * 64: 416 + (kc + 1) * 64],
                               kc == 0, kc == 1, [hwb[s], Ab[3 + kc][t16 // 4]], [bP[1]], (i8 == 7 and kc == 1))
                    yield
                    vcol = 0 if s == 0 else 64
                    base = MV0 + s * 2048 + g8 * 8 * 128 + vcol
                    dst = BU[:, base: base + 8 * 128].rearrange("p (a b) -> p a b", b=128)[:, :, 0:64]
                    src = psP[1][:, :].rearrange("p (a b) -> p a b", b=64)
                    copy_("dve", dst, src, [bP[1]], [mvb[s]])
                    yield

                lane_a = [g_q(tt) for tt in range(4)]
                lane_b = [g_k(tt) for tt in range(4)] + [g_v(0), g_v(1)]
                return [lane_a, lane_b]

            def unit(h):
                s = h % 2
                c = h // 2
                r0 = 0 if s == 0 else 64
                return dict(
                    q=(lambda qc: MQ[s][:, qc * 512:(qc + 1) * 512]),
                    qb=(lambda qc: [mqb[s][qc]]),
                    k=(lambda kt: MK[s][:, kt * 128:(kt + 1) * 128]),
                    kb=(lambda kt: [mkb[s][kt // 4]]),
                    v=(lambda kt: BU[:, MV0 + s * 2048 + kt * 128: MV0 + s * 2048 + (kt + 1) * 128]),
                    vb=[mvb[s]],
                    top=(s == 0),
                    out=(lambda qc: (Bv(c, qc * 512, 512, r0, r0 + 64), [Bb[c][qc]])),
                )

            if DBG_STOP < 2:
                return
            load_hw(0)
            load_hw(1)
            sp0 = Sprinkle(proj_gens(0))
            sp0.drain()
            for h in range(16):
                spr = Sprinkle(proj_gens(h + 1) if h + 1 < 16 else [], every=1)
                if h + 2 < 16:
                    load_hw(h + 2)
                attention([unit(h)], spr, 96 ** -0.5)
            flush_deferred()
            if DBG_STOP < 3:
                return
            wo_phase(li, MLA_WO, lambda c, t0, n: Bv(c, t0, n), Bb)

        P.barrier([(("d", "cm"), 16), (("d", "par"), 16)])
        for sq_i in range(n_seq):
            for kc in range(8):
                dma("sp", xT[:, kc * S:(kc + 1) * S], xin[sq_i, kc * 128:(kc + 1) * 128, :], f"xl{kc}", [], xb[kc])
            for li in layers:
                if li % 2 == 0:
                    mla(li)
                else:
                    gqa(li)
                P.barrier()
                last = (li == layers[-1])
                if DBG_STOP >= 4:
                    ffn(li, INCR_NORM and not last, sq_i if last else None)
                P.barrier()
            if not layers or DBG_STOP < 4:
                for kc in range(8):
                    dma("sp", yout[sq_i, kc * 128:(kc + 1) * 128, :], xT[:, kc * S:(kc + 1) * S], f"xs{kc}", xb[kc], [])
        P.wait_only("sp", [(("d", f"xs{kc}"), P.dcnt[("d", f"xs{kc}")]) for kc in range(8)])

        sems = {}
        for k in P.inckeys:
            sems[k] = es.enter_context(nc.semaphore("s_" + "_".join(str(t) for t in k)))
        with nc.Block() as block:
            def replay(e, ops):
                for (waits, fn, inc, amt) in ops:
                    for (k, v) in waits:
                        e.wait_ge(sems[k], v)
                    if fn is not None:
                        ins = fn(e)
                        if inc is not None:
                            ins.then_inc(sems[inc], amt)

            @block.sync
            def _(e):
                replay(e, P.ops["sp"])

            @block.tensor
            def _(e):
                replay(e, P.ops["pe"])

            @block.scalar
            def _(e):
                replay(e, P.ops["act"])

            @block.vector
            def _(e):
                replay(e, P.ops["dve"])

            @block.gpsimd
            def _(e):
                replay(e, P.ops["pool"])
    return nc, P


_CONSTS = None


def kernel(**inputs):
    global _CONSTS
    inp = {k: np.asarray(v) for k, v in inputs.items()}
    x = inp["x"].astype(np.float32, copy=False)
    wb, par = _host_layout(inp)
    if _CONSTS is None:
        _CONSTS = _host_consts()
    cm, tab = _CONSTS
    tab2 = np.ascontiguousarray(tab.reshape(2, 128, 2 * S))
    nseq = x.shape[0] // NCORE
    nc, _ = build_nc(nseq, (0, 1, 2, 3))
    in_maps = []
    for c in range(NCORE):
        xs = np.ascontiguousarray(x[c * nseq:(c + 1) * nseq].transpose(0, 2, 1))
        in_maps.append({"xT": xs, "wblob": wb, "params": par, "cmat": cm, "tabs": tab2})
    res = run_bass_kernel_spmd(nc, in_maps, core_ids=list(range(NCORE)))
    outs = [np.asarray(r["yT"]).transpose(0, 2, 1) for r in res.results]
    return np.ascontiguousarray(np.concatenate(outs, axis=0)).astype(np.float32, copy=False)
```

```python
import numpy as np
import ml_dtypes
from contextlib import ExitStack
import concourse.bass as bass
import concourse.mybir as mybir
from concourse.bass_utils import run_bass_kernel_spmd

F32 = mybir.dt.float32
BF16 = mybir.dt.bfloat16
ALU = mybir.AluOpType
AF = mybir.ActivationFunctionType

D = 1024
S = 2048
NCORE = 8
DFF = 2816
NJ = DFF // 128
EPS = 1e-6
DEPTH = 4
EH = [0, 1, 2, 3, 8, 9, 10, 11]
OH = [4, 5, 6, 7, 12, 13, 14, 15]

MLA_WIN = 0
MLA_WKR = MLA_WIN + 8 * 640
MLA_WH = MLA_WKR + 8 * 96
MLA_WO = MLA_WH + 16 * 544
MLA_END = MLA_WO + 8 * 1024
GQA_WQ = 0
GQA_WK = GQA_WQ + 8 * 1024
GQA_WV = GQA_WK + 2 * 1024
GQA_WO = GQA_WV + 8 * 256
GQA_END = GQA_WO + 8 * 1024
ATT_END = max(MLA_END, GQA_END)
FFN_GU = ATT_END
FFN_DN = FFN_GU + NJ * 2048
XL = FFN_DN + 8 * NJ * 128

def _pcols():
    off = 0
    cols = {}
    for j in range(2):
        for nm, n in (("mla_norm", 8), ("q_lora", 3), ("kv_lora", 2), ("mla_q", 1), ("mla_kn", 1), ("mla_kr", 1)):
            cols[(nm, j)] = off
            off += n
    for j in range(2):
        for nm, n in (("gqa_norm", 8), ("gqa_q", 1), ("gqa_k", 1)):
            cols[(nm, j)] = off
            off += n
    for i in range(4):
        cols[("ffn_norm", i)] = off
        off += 8
    return cols, off
PCOL, NPAR = _pcols()

C_ONES, C_B128, C_B96, C_R128, C_R96 = 0, 128, 256, 384, 512


def _host_consts():
    cm = np.zeros((128, 5 * 128), np.float32)
    cm[:, C_ONES:C_ONES + 128] = 1.0
    cm[0:64, C_B128:C_B128 + 64] = 1.0 / 64
    cm[64:128, C_B128 + 64:C_B128 + 128] = 1.0 / 64
    cm[0:64, C_B96:C_B96 + 64] = 1.0 / 64
    cm[64:96, C_B96 + 64:C_B96 + 96] = 1.0 / 32
    for hb in (0, 64):
        for blk in (0, 32):
            for i in range(32):
                dst = hb + blk + i
                if i < 16:
                    cm[dst + 16, C_R128 + dst] = -1.0
                else:
                    cm[dst - 16, C_R128 + dst] = 1.0
    for blk in (0, 16):
        for i in range(16):
            dst = 64 + blk + i
            if i < 8:
                cm[dst + 8, C_R96 + dst] = -1.0
            else:
                cm[dst - 8, C_R96 + dst] = 1.0
    t = np.arange(S)
    row = (t // 64).astype(np.float32)
    col = (t % 64).astype(np.float32)
    tab = np.zeros((2, 128, 2, S), np.float32)
    tab[0, :, 0, :] = 1.0
    inv8 = (np.float32(10000.0) ** (-np.arange(0, 16, 2, dtype=np.float32) / np.float32(16))).astype(np.float32)
    for blk, pos in ((0, row), (16, col)):
        for i in range(16):
            ang = (pos * inv8[i % 8]).astype(np.float32)
            tab[0, 64 + blk + i, 0, :] = np.cos(ang)
            tab[0, 64 + blk + i, 1, :] = np.sin(ang)
    inv16 = (np.float32(10000.0) ** (-np.arange(0, 32, 2, dtype=np.float32) / np.float32(32))).astype(np.float32)
    for hb in (0, 64):
        for blk, pos in ((0, row), (32, col)):
            for i in range(32):
                ang = (pos * inv16[i % 16]).astype(np.float32)
                tab[1, hb + blk + i, 0, :] = np.cos(ang)
                tab[1, hb + blk + i, 1, :] = np.sin(ang)
    return cm.astype(ml_dtypes.bfloat16), tab.astype(ml_dtypes.bfloat16)


def _host_layout(inp):
    wb = np.zeros((DEPTH, 128, XL), np.float32)
    par = np.zeros((128, NPAR), np.float32)

    def kc_slab(w, cols):
        k = w.shape[0] // 128
        return w[:, cols].reshape(k, 128, len(cols)).transpose(1, 0, 2).reshape(128, -1)

    for i in range(DEPTH):
        j = i // 2
        if i % 2 == 0:
            w_in = inp["mla_w_in"][j]
            wb[i, :, MLA_WIN:MLA_WIN + 8 * 640] = kc_slab(w_in, np.arange(640))
            kr = np.zeros((1024, 96), np.float32)
            kr[:, 64:96] = w_in[:, 640:672]
            wb[i, :, MLA_WKR:MLA_WKR + 8 * 96] = kc_slab(kr, np.arange(96))
            wuq = inp["mla_w_uq"][j]
            wukv = inp["mla_w_ukv"][j]
            for h in range(16):
                o = MLA_WH + h * 544
                wb[i, :, o:o + 288] = kc_slab(wuq, np.arange(h * 96, h * 96 + 96))
                wb[i, :, o + 288:o + 416] = kc_slab(wukv, np.arange(h * 128, h * 128 + 64))
                wb[i, :, o + 416:o + 544] = kc_slab(wukv, np.arange(h * 128 + 64, h * 128 + 128))
            wo = inp["mla_w_o"][j]
            for m in range(8):
                wb[i, :, MLA_WO + m * 1024:MLA_WO + (m + 1) * 1024] = kc_slab(wo, np.arange(m * 128, m * 128 + 128))
            par[:, PCOL[("mla_norm", j)]:PCOL[("mla_norm", j)] + 8] = inp["mla_norm"][j].reshape(8, 128).T
            par[:, PCOL[("q_lora", j)]:PCOL[("q_lora", j)] + 3] = inp["mla_q_lora_norm"][j].reshape(3, 128).T
            par[:, PCOL[("kv_lora", j)]:PCOL[("kv_lora", j)] + 2] = inp["mla_kv_lora_norm"][j].reshape(2, 128).T
            par[0:96, PCOL[("mla_q", j)]] = inp["mla_q_norm"][j]
            par[0:64, PCOL[("mla_kn", j)]] = inp["mla_k_norm"][j][:64]
            par[64:96, PCOL[("mla_kr", j)]] = inp["mla_k_norm"][j][64:96]
        else:
            wqkv = inp["gqa_w_qkv"][j]
            for c in range(8):
                cols = np.concatenate([np.arange(EH[c] * 64, EH[c] * 64 + 64), np.arange(OH[c] * 64, OH[c] * 64 + 64)])
                wb[i, :, GQA_WQ + c * 1024:GQA_WQ + (c + 1) * 1024] = kc_slab(wqkv, cols)
            for t in range(2):
                wb[i, :, GQA_WK + t * 1024:GQA_WK + (t + 1) * 1024] = kc_slab(wqkv, np.arange(1024 + t * 128, 1024 + t * 128 + 128))
            wb[i, :, GQA_WV:GQA_WV + 2048] = kc_slab(wqkv, np.arange(1280, 1536))
            wo = inp["gqa_w_o"][j]
            rows = np.concatenate([np.concatenate([np.arange(EH[c] * 64, EH[c] * 64 + 64), np.arange(OH[c] * 64, OH[c] * 64 + 64)]) for c in range(8)])
            wop = wo[rows, :]
            for m in range(8):
                wb[i, :, GQA_WO + m * 1024:GQA_WO + (m + 1) * 1024] = kc_slab(wop, np.arange(m * 128, m * 128 + 128))
            par[:, PCOL[("gqa_norm", j)]:PCOL[("gqa_norm", j)] + 8] = inp["gqa_norm"][j].reshape(8, 128).T
            par[:, PCOL[("gqa_q", j)]] = np.tile(inp["gqa_q_norm"][j], 2)
            par[:, PCOL[("gqa_k", j)]] = np.tile(inp["gqa_k_norm"][j], 2)
        wgu = inp["ffn_w_gate_up"][i]
        for jj in range(NJ):
            cols = np.concatenate([np.arange(jj * 128, jj * 128 + 128), np.arange(DFF + jj * 128, DFF + jj * 128 + 128)])
            wb[i, :, FFN_GU + jj * 2048:FFN_GU + (jj + 1) * 2048] = kc_slab(wgu, cols)
        wdn = inp["ffn_w_down"][i]
        for m in range(8):
            wb[i, :, FFN_DN + m * NJ * 128:FFN_DN + (m + 1) * NJ * 128] = kc_slab(wdn, np.arange(m * 128, m * 128 + 128))
        par[:, PCOL[("ffn_norm", i)]:PCOL[("ffn_norm", i)] + 8] = inp["ffn_norm"][i].reshape(8, 128).T
    return wb, par


class Buf:
    __slots__ = ("name", "w", "r", "excl")

    def __init__(self, name, excl=False):
        self.name = name
        self.w = None
        self.r = {}
        self.excl = excl


ENGS = ("pe", "act", "dve", "pool", "sp")
LIMIT = 16000


class Plan:
    def __init__(self):
        self.ops = {e: [] for e in ENGS}
        self.cnt = {e: 0 for e in ENGS}
        self.gen = {e: 0 for e in ENGS}
        self.waited = {e: {} for e in ENGS}
        self.dcnt = {}
        self.inckeys = []
        self._incset = set()

    def _addkey(self, k):
        if k not in self._incset:
            self._incset.add(k)
            self.inckeys.append(k)

    def _collect(self, reads, writes):
        deps = []
        for b in reads:
            if b.w is not None:
                deps.append(b.w)
        for b in writes:
            if b.w is not None:
                deps.append(b.w)
            for k, v in b.r.items():
                if k[0] == "e":
                    deps.append((("e", k[1], v[0]), v[1]))
                else:
                    deps.append((k, v))
        return deps

    def _need(self, eng, deps):
        out = {}
        wd = self.waited[eng]
        for key, val in deps:
            if key[0] == "e":
                if key[1] == "pe" and eng == "pe":
                    continue
                k2 = ("e", key[1])
                cur = wd.get(k2)
                if cur is not None and cur >= (key[2], val):
                    continue
                prev = out.get(k2)
                if prev is None or (key[2], val) > prev:
                    out[k2] = (key[2], val)
            else:
                if wd.get(key, 0) >= val:
                    continue
                if out.get(key, 0) < val:
                    out[key] = val
        waits = []
        for k, v in out.items():
            wd[k] = v
            if k[0] == "e":
                waits.append((("e", k[1], v[0]), v[1]))
            else:
                waits.append((k, v))
        return waits

    def _mark(self, tk, reads, writes):
        key, val = tk
        for b in reads:
            if key[0] == "e":
                k2 = ("e", key[1])
                nv = (key[2], val)
                if b.r.get(k2, (-1, -1)) < nv:
                    b.r[k2] = nv
            else:
                if b.r.get(key, 0) < val:
                    b.r[key] = val
        for b in writes:
            b.w = tk
            b.r = {}

    def op(self, eng, fn, reads=(), writes=(), signal=True):
        if any(b.excl for b in reads):
            writes = list(writes) + [b for b in reads if b.excl]
            reads = [b for b in reads if not b.excl]
        waits = self._need(eng, self._collect(reads, writes))
        if signal:
            if self.cnt[eng] >= LIMIT:
                self.gen[eng] += 1
                self.cnt[eng] = 0
            self.cnt[eng] += 1
            key = ("e", eng, self.gen[eng])
            tk = (key, self.cnt[eng])
            self._addkey(key)
            self.ops[eng].append((waits, fn, key, 1))
        else:
            g, c = self.gen[eng], self.cnt[eng]
            if c >= LIMIT:
                g += 1
                c = 0
            tk = (("e", eng, g), c + 1)
            self.ops[eng].append((waits, fn, None, 0))
        self._mark(tk, reads, writes)
        return tk

    def dma(self, q, fn, semid, reads=(), writes=()):
        waits = self._need(q, self._collect(reads, writes))
        key = ("d", semid)
        self.dcnt[key] = self.dcnt.get(key, 0) + 16
        tk = (key, self.dcnt[key])
        self._addkey(key)
        self.ops[q].append((waits, fn, key, 16))
        self._mark(tk, reads, writes)
        return tk

    def last_tickets(self):
        tks = []
        for e in ("pe", "act", "dve", "pool"):
            if self.gen[e] > 0 or self.cnt[e] > 0:
                tks.append((("e", e, self.gen[e]), self.cnt[e]))
        return tks

    def barrier(self, extra=()):
        tks = self.last_tickets() + list(extra)
        for e in ENGS:
            waits = self._need(e, tks)
            if waits:
                self.ops[e].append((waits, None, None, 0))

    def wait_only(self, eng, tks):
        waits = self._need(eng, tks)
        if waits:
            self.ops[eng].append((waits, None, None, 0))


class Sprinkle:
    def __init__(self, lanes, every=1):
        self.lanes = [list(l) for l in lanes]
        self.every = every
        self.i = 0

    def tick(self):
        self.i += 1
        if self.i % self.every == 0:
            self.step()

    def step(self):
        for l in self.lanes:
            while l:
                try:
                    next(l[0])
                    break
                except StopIteration:
                    l.pop(0)

    def drain(self):
        while any(self.lanes):
            self.step()


DBG_STOP = 99
SQ_POOL = True
INCR_NORM = True
POOL = "dve"


def build_nc(n_seq=4, layers=(0, 1, 2, 3), dbg=False):
    nc = bass.Bass("TRN2", target_bir_lowering=False)
    xin = nc.dram_tensor("xT", [n_seq, D, S], F32, kind="ExternalInput").ap()
    wfp = nc.dram_tensor("wblob", [DEPTH, 128, XL], F32, kind="ExternalInput").ap()
    par_d = nc.dram_tensor("params", [128, NPAR], F32, kind="ExternalInput").ap()
    cm_d = nc.dram_tensor("cmat", [128, 640], BF16, kind="ExternalInput").ap()
    tab_d = nc.dram_tensor("tabs", [2, 128, 2 * S], BF16, kind="ExternalInput").ap()
    wbf = nc.dram_tensor("wbf", [DEPTH, 128, XL], BF16, kind="Internal").ap()
    yout = nc.dram_tensor("yT", [n_seq, D, S], F32, kind="ExternalOutput").ap()

    P = Plan()
    with ExitStack() as es:
        def sb(name, shape, dt):
            return es.enter_context(nc.sbuf_tensor(name, shape, dt))

        def ps(name, shape):
            return es.enter_context(nc.psum_tensor(name, shape, F32))

        xT = sb("xT_sb", [128, 8 * S], F32)
        A = sb("A_sb", [128, 8 * S], BF16)
        BU = sb("BU_sb", [128, 32768], BF16)
        T32 = sb("T32_sb", [128, 4096], F32)
        TAB = sb("TAB_sb", [128, 2 * S], BF16)
        WT = sb("WT_sb", [128, 9280], BF16)
        CM = sb("CM_sb", [128, 640], BF16)
        PAR = sb("PAR_sb", [128, NPAR], F32)
        CMG = sb("CMG_sb", [128, 256], BF16)
        GT = sb("GT_sb", [128, 4], F32)
        cmgb = Buf("cmg")
        gtb = Buf("gt")
        psS = [ps("psS0", [128, 1024]), ps("psS1", [128, 1024])]
        psO = [ps("psO0", [128, 512]), ps("psO1", [128, 512])]
        psP = [ps("psP1", [128, 512]), ps("psP2", [128, 512])]

        xb = [[Buf(f"x{kc}_{tt}") for tt in range(4)] for kc in range(8)]
        Ab = [[Buf(f"A{kc}_{tt}") for tt in range(4)] for kc in range(8)]
        Bb = [[Buf(f"B{kc}_{tt}") for tt in range(4)] for kc in range(8)]
        bS = [[Buf("S0a", True), Buf("S0b", True)], [Buf("S1a", True), Buf("S1b", True)]]
        bO = [Buf("O0", True), Buf("O1", True)]
        bP = [Buf("P1", True), Buf("P2", True)]
        tabb = Buf("tab")
        cmb = Buf("cm")
        parb = Buf("par")
        sqb = [Buf("sq0"), Buf("sq1")]
        qgb = [Buf("qg0"), Buf("qg1")]
        rstdb = [Buf("rstd0"), Buf("rstd1")]
        ab = [Buf("a0"), Buf("a1")]
        bb = [Buf("b0"), Buf("b1")]
        rdenb = [Buf("rden0"), Buf("rden1")]
        ptb = [Buf("pt0"), Buf("pt1")]
        wreg = {}

        def xv(kc, tt):
            return xT[:, kc * S + tt * 512: kc * S + (tt + 1) * 512]

        def Av(kc, t0, n, r0=0, r1=128):
            return A[r0:r1, kc * S + t0: kc * S + t0 + n]

        def Bv(kc, t0, n, r0=0, r1=128):
            return BU[r0:r1, kc * S + t0: kc * S + t0 + n]

        U0 = 16384
        PT = [BU[:, U0 + 12288 + s * 1024: U0 + 12288 + (s + 1) * 1024] for s in range(2)]
        SQ = [BU[:, U0 + 14336 + s * 512: U0 + 14336 + (s + 1) * 512] for s in range(2)]
        QG = [BU[:, U0 + 15360 + s * 512: U0 + 15360 + (s + 1) * 512] for s in range(2)]
        RSTD = [T32[:, s * 512:(s + 1) * 512] for s in range(2)]
        TA = [T32[:, 1024 + s * 512: 1024 + (s + 1) * 512] for s in range(2)]
        TB = [T32[:, 2048 + s * 512: 2048 + (s + 1) * 512] for s in range(2)]
        RDEN = [T32[:, 3072 + s * 512: 3072 + (s + 1) * 512] for s in range(2)]
        ones_ap = CM[:, C_ONES:C_ONES + 128]

        def cosv(t0, n, R):
            return TAB[0:R, t0:t0 + n]

        def sinv(t0, n, R):
            return TAB[0:R, S + t0:S + t0 + n]

        def pcol(name, idx, k=0, R=128, r0=0):
            c = PCOL[(name, idx)] + k
            return PAR[r0:R, c:c + 1]

        def mm(out, lhsT, rhs, start, stop, reads, writes, signal):
            P.op("pe", lambda e: e.matmul(out, lhsT=lhsT, rhs=rhs, start=start, stop=stop), reads, writes, signal)

        def act(out, in_, func, reads, writes, scale=1.0, bias=0.0):
            P.op("act", lambda e: e.activation(out=out, in_=in_, func=func, bias=bias, scale=scale), reads, writes)

        def tt_(eng, out, in0, in1, op, reads, writes):
            P.op(eng, lambda e: e.tensor_tensor(out=out, in0=in0, in1=in1, op=op), reads, writes)

        def ts_(eng, out, in0, scalar, op, reads, writes):
            P.op(eng, lambda e: e.tensor_scalar(out=out, in0=in0, scalar1=scalar, scalar2=None, op0=op), reads, writes)

        def stt_(eng, out, in0, scalar, in1, reads, writes):
            P.op(eng, lambda e: e.scalar_tensor_tensor(out=out, in0=in0, scalar=scalar, in1=in1, op0=ALU.mult, op1=ALU.mult), reads, writes)

        def copy_(eng, out, in_, reads, writes):
            P.op(eng, lambda e: e.tensor_copy(out=out, in_=in_), reads, writes)

        def dma(q, out, in_, semid, reads, writes):
            P.dma(q, lambda e: e.dma_start(out=out, in_=in_), semid, reads, writes)

        dma("sp", CM[:], cm_d, "cm", [], [cmb])
        dma("sp", PAR[:], par_d, "par", [], [parb])
        CH = 4096
        for li in layers:
            for (rname, c0, c1) in (("att", 0, ATT_END), ("gu", FFN_GU, FFN_DN), ("dn", FFN_DN, XL)):
                rb = Buf(f"w{li}{rname}")
                wreg[(li, rname)] = rb
                c = c0
                while c < c1:
                    n = min(CH, c1 - c)
                    dma("pool", wbf[li, :, c:c + n], wfp[li, :, c:c + n], f"cast{li}{rname}", [], [])
                    c += n
                key = ("d", f"cast{li}{rname}")
                rb.w = (key, P.dcnt[key])

        def rmsnorm_main(gname, gidx):
            for tt in range(4):
                for kc in range(8):
                    s = kc % 2
                    act(SQ[s], xv(kc, tt), AF.Square, [xb[kc][tt]], [sqb[s]])
                    mm(psP[1][:, :], ones_ap, SQ[s], kc == 0, kc == 7, [sqb[s], cmb], [bP[1]], True)
                act(RSTD[0], psP[1][:, :], AF.Ln, [bP[1]], [rstdb[0]], scale=1.0 / D, bias=EPS)
                act(RSTD[0], RSTD[0], AF.Exp, [rstdb[0]], [rstdb[0]], scale=-0.5)
                for kc in range(8):
                    stt_("dve", Av(kc, tt * 512, 512), xv(kc, tt), pcol(gname, gidx, kc), RSTD[0],
                         [xb[kc][tt], rstdb[0], parb], [Ab[kc][tt]])

        T32b = T32.bitcast(BF16)
        NSQ = [T32b[:, 6144 + s_ * 512: 6144 + (s_ + 1) * 512] for s_ in range(4)]
        nsqb = [Buf(f"nsq{s_}") for s_ in range(4)]
        NRSTD = [RSTD[0], RSTD[1], TB[0], TB[1]]
        nrstdb = [rstdb[0], rstdb[1], bb[0], bb[1]]
        nsq_i = [0]
        have_rstd = [False]

        def norm_stat(m, tt, bank_ap, bank_buf):
            s_ = nsq_i[0] % 4
            nsq_i[0] += 1
            act(NSQ[s_], xv(m, tt), AF.Square, [xb[m][tt]], [nsqb[s_], rdenb[s_ // 2]])
            mm(bank_ap, ones_ap, NSQ[s_], m == 0, m == 7, [nsqb[s_], rdenb[s_ // 2], cmb], [bank_buf], True)

        def norm_finish(tt, bank_ap, bank_buf):
            act(NRSTD[tt], bank_ap, AF.Ln, [bank_buf], [nrstdb[tt]], scale=1.0 / D, bias=EPS)
            act(NRSTD[tt], NRSTD[tt], AF.Exp, [nrstdb[tt]], [nrstdb[tt]], scale=-0.5)

        def norm_apply_one(gname, gidx, tt, kc):
            stt_("dve", Av(kc, tt * 512, 512), xv(kc, tt), pcol(gname, gidx, kc), NRSTD[tt],
                 [xb[kc][tt], nrstdb[tt], parb], [Ab[kc][tt]])

        def do_norm(gname, gidx, defer_from=4):
            later = []
            if have_rstd[0]:
                for tt in range(4):
                    for kc in range(8):
                        if tt < defer_from:
                            norm_apply_one(gname, gidx, tt, kc)
                        else:
                            later.append(lambda tt=tt, kc=kc: norm_apply_one(gname, gidx, tt, kc))
                have_rstd[0] = False
            else:
                rmsnorm_main(gname, gidx)
            return later

        def head_tile(pbank, pbuf, R, gain_ap, bones_ap, rot_ap, t0, outs, slot, proj_fn, qg_act=False, sq_pool=False):
            proj_fn()
            yield
            ps_ap = pbank[0:R, :]
            if sq_pool:
                ts_("dve", QG[slot][0:R, :], ps_ap, gain_ap, ALU.mult, [pbuf, parb], [qgb[slot]])
                yield
                tt_("pool", SQ[slot][0:R, :], QG[slot][0:R, :], QG[slot][0:R, :], ALU.mult, [qgb[slot]], [sqb[slot]])
                yield
            else:
                act(SQ[slot][0:R, :], ps_ap, AF.Square, [pbuf], [sqb[slot]])
                yield
            if sq_pool:
                pass
            elif qg_act:
                P.op("act", lambda e: e.activation(out=QG[slot][0:R, :], in_=ps_ap, func=AF.Identity, bias=0.0, scale=gain_ap),
                     [pbuf, parb], [qgb[slot]])
            else:
                ts_("dve", QG[slot][0:R, :], ps_ap, gain_ap, ALU.mult, [pbuf, parb], [qgb[slot]])
            if not sq_pool:
                yield
            mm(ps_ap, bones_ap, SQ[slot][0:R, :], True, True, [sqb[slot], cmb, cmgb], [pbuf], True)
            yield
            act(RSTD[slot][0:R, :], ps_ap, AF.Ln, [pbuf], [rstdb[slot]], scale=1.0, bias=EPS)
            act(RSTD[slot][0:R, :], RSTD[slot][0:R, :], AF.Exp, [rstdb[slot]], [rstdb[slot]], scale=-0.5)
            yield
            if rot_ap is not None:
                mm(ps_ap, rot_ap, QG[slot][0:R, :], True, True, [qgb[slot], cmb], [pbuf], True)
                tt_("dve", TA[slot][0:R, :], QG[slot][0:R, :], cosv(t0, 512, R), ALU.mult, [qgb[slot], tabb], [ab[slot]])
                yield
                tt_("dve", TB[slot][0:R, :], ps_ap, sinv(t0, 512, R), ALU.mult, [pbuf, tabb], [bb[slot]])
                tt_("dve", TA[slot][0:R, :], TA[slot][0:R, :], TB[slot][0:R, :], ALU.add, [ab[slot], bb[slot]], [ab[slot]])
                for (oap, obufs, r0, r1) in outs:
                    tt_("dve", oap, TA[slot][r0:r1, :], RSTD[slot][r0:r1, :], ALU.mult, [ab[slot], rstdb[slot]], obufs)
            else:
                for (oap, obufs, r0, r1) in outs:
                    tt_("dve", oap, QG[slot][r0:r1, :], RSTD[slot][r0:r1, :], ALU.mult, [qgb[slot], rstdb[slot]], obufs)
            yield

        def run_lanes(lanes):
            lanes = [list(l) for l in lanes]
            while any(lanes):
                for l in lanes:
                    while l:
                        try:
                            next(l[0])
                            break
                        except StopIteration:
                            l.pop(0)

        def run(gen):
            for _ in gen:
                pass

        deferred = []

        def push_norm(obank, obuf, rden, rdbuf, o0, oap, obufs):
            d0 = 64 - o0
            for c0 in range(0, 512, 128):
                deferred.append(lambda c0=c0: P.op(
                    "dve", (lambda e: e.reciprocal(out=rden[o0:o0 + 64, c0:c0 + 128], in_=obank[d0:d0 + 64, c0:c0 + 128])),
                    [obuf], [rdbuf]))
            deferred.append(lambda: tt_("dve", oap, obank[o0:o0 + 64, :], rden[o0:o0 + 64, :], ALU.mult, [obuf, rdbuf], obufs))

        def pop_deferred():
            if deferred:
                deferred.pop(0)()

        def flush_deferred():
            while deferred:
                deferred.pop(0)()

        def attention(units, sprinkle, scale):
            iters = [(u, qc, kp) for u in units for qc in range(4) for kp in range(8)]

            def qk(i):
                u, qc, kp = iters[i]
                s = i % 2
                for hh in range(2):
                    kt = 2 * kp + hh
                    mm(psS[s][:, hh * 512:(hh + 1) * 512], u["k"](kt), u["q"](qc), True, True,
                       u["kb"](kt) + u["qb"](qc), [bS[s][hh]], hh == 1)

            qk(0)
            for i, (u, qc, kp) in enumerate(iters):
                s = i % 2
                if i + 1 < len(iters):
                    qk(i + 1)
                act(PT[s], psS[s][:, :], AF.Exp, [bS[s][0], bS[s][1]], [ptb[s]], scale=scale)
                n = i // 8
                ob = n % 2
                for hh in range(2):
                    kt = 2 * kp + hh
                    mm(psO[ob][:, :], u["v"](kt), PT[s][:, hh * 512:(hh + 1) * 512],
                       kp == 0 and hh == 0, kp == 7 and hh == 1, [ptb[s]] + u["vb"], [bO[ob]], hh == 1)
                if kp == 7:
                    o0 = 0 if u["top"] else 64
                    oap, obufs = u["out"](qc)
                    push_norm(psO[ob], bO[ob], RDEN[ob], rdenb[ob], o0, oap, obufs)
                pop_deferred()
                sprinkle.tick()
            sprinkle.drain()

        WO_SLOT = [WT[:, s * 1024:(s + 1) * 1024] for s in range(2)]
        wob = [Buf("wo0"), Buf("wo1")]

        def wo_phase(li, wo_off, src_v, src_b):
            pend = []
            for m in range(8):
                s = m % 2
                dma("sp", WO_SLOT[s], wbf[li, :, wo_off + m * 1024: wo_off + (m + 1) * 1024], f"wo{s}",
                    [wreg[(li, "att")]], [wob[s]])
                for tt in range(4):
                    ob = tt % 2
                    for c in range(8):
                        mm(psO[ob][:, :], WO_SLOT[s][:, c * 128:(c + 1) * 128], src_v(c, tt * 512, 512), c == 0, c == 7,
                           [wob[s], src_b[c][tt]], [bO[ob]], c == 7)
                    tt_("dve", xv(m, tt), xv(m, tt), psO[ob][:, :], ALU.add, [xb[m][tt], bO[ob]], [xb[m][tt]])
                    if INCR_NORM:
                        pend.append((m, tt, psS[tt // 2][:, (tt % 2) * 512:(tt % 2 + 1) * 512], bS[tt // 2][tt % 2]))
                        if len(pend) > 2:
                            norm_stat(*pend.pop(0))
            if INCR_NORM:
                while pend:
                    norm_stat(*pend.pop(0))
                for tt in range(4):
                    norm_finish(tt, psS[tt // 2][:, (tt % 2) * 512:(tt % 2 + 1) * 512], bS[tt // 2][tt % 2])
                have_rstd[0] = True

        ACTT = lambda j, t0, n: BU[:, j * 1024 + t0: j * 1024 + t0 + n]
        WGU = [BU[:, 22528 + s * 2048: 22528 + (s + 1) * 2048] for s in range(2)]
        WDN = [BU[:, 26624 + s * 2816: 26624 + (s + 1) * 2816] for s in range(2)]
        wgub = [Buf("wgu0"), Buf("wgu1")]
        wdnb = [Buf("wdn0"), Buf("wdn1")]
        actb = [[Buf(f"act{j}_{t}") for t in range(2)] for j in range(NJ)]

        def ffn(li, stats, store_seq):
            pend = []
            later = do_norm("ffn_norm", li, defer_from=2)
            for half in range(2):
                for j in range(NJ):
                    if later:
                        later.pop(0)()
                    s = j % 2
                    dma("sp", WGU[s], wbf[li, :, FFN_GU + j * 2048: FFN_GU + (j + 1) * 2048], f"wgu{s}",
                        [wreg[(li, "gu")]], [wgub[s]])
                    for t2 in range(2):
                        tt = half * 2 + t2
                        for kc in range(8):
                            mm(psS[0][:, t2 * 512:(t2 + 1) * 512], WGU[s][:, kc * 256: kc * 256 + 128], Av(kc, tt * 512, 512),
                               kc == 0, kc == 7, [wgub[s], Ab[kc][tt]], [bS[0][t2]], kc == 7)
                        for kc in range(8):
                            mm(psS[1][:, t2 * 512:(t2 + 1) * 512], WGU[s][:, kc * 256 + 128: kc * 256 + 256], Av(kc, tt * 512, 512),
                               kc == 0, kc == 7, [wgub[s], Ab[kc][tt]], [bS[1][t2]], kc == 7)
                        act(TA[t2], psS[0][:, t2 * 512:(t2 + 1) * 512], AF.Silu, [bS[0][t2]], [ab[t2]])
                        tt_("dve", ACTT(j, t2 * 512, 512), TA[t2], psS[1][:, t2 * 512:(t2 + 1) * 512], ALU.mult,
                            [ab[t2], bS[1][t2]], [actb[j][t2]])
                while later:
                    later.pop(0)()
                for m in range(8):
                    s = m % 2
                    dma("sp", WDN[s], wbf[li, :, FFN_DN + m * 2816: FFN_DN + (m + 1) * 2816], f"wdn{s}",
                        [wreg[(li, "dn")]], [wdnb[s]])
                    for t2 in range(2):
                        tt = half * 2 + t2
                        for j in range(NJ):
                            mm(psO[t2][:, :], WDN[s][:, j * 128:(j + 1) * 128], ACTT(j, t2 * 512, 512), j == 0, j == NJ - 1,
                               [wdnb[s], actb[j][t2]], [bO[t2]], j == NJ - 1)
                        tt_("dve", xv(m, tt), xv(m, tt), psO[t2][:, :], ALU.add, [xb[m][tt], bO[t2]], [xb[m][tt]])
                        if stats:
                            pend.append((m, tt, psP[t2][:, :], bP[t2]))
                            if len(pend) > 2:
                                norm_stat(*pend.pop(0))
                    if store_seq is not None and half == 1:
                        dma("act", yout[store_seq, m * 128:(m + 1) * 128, :], xT[:, m * S:(m + 1) * S], f"xs{m}", xb[m], [])
                if stats:
                    while pend:
                        norm_stat(*pend.pop(0))
                    for t2 in range(2):
                        norm_finish(half * 2 + t2, psP[t2][:, :], bP[t2])
            if stats:
                have_rstd[0] = True

        GK = [BU[:, U0 + i * 2048: U0 + (i + 1) * 2048] for i in range(2)]
        gkb = [[Buf(f"gk{i}_{tt}") for tt in range(4)] for i in range(2)]
        GV0 = U0 + 4096
        gvb = [Buf(f"gv{j}") for j in range(4)]
        GW = [WT[:, 2048 + s * 1024: 2048 + (s + 1) * 1024] for s in range(3)]
        gwb = [Buf(f"gw{s}") for s in range(3)]
        GWV = WT[:, 5120:5120 + 2048]
        gwvb = Buf("gwv")

        def gqa(li):
            j = li // 2
            dma("sp", TAB[:], tab_d[1], "tab", [], [tabb])
            do_norm("gqa_norm", j)
            P.op(POOL, lambda e: e.memset(BU[:, GV0:GV0 + 8192], 1.0), [], gvb)
            wcount = [0]

            def load_w(off):
                s = wcount[0] % 3
                wcount[0] += 1
                dma("sp", GW[s], wbf[li, :, off:off + 1024], f"gw{s}", [wreg[(li, "att")]], [gwb[s]])
                return s

            lane_banks = [(psP[0], bP[0]), (psP[1], bP[1])]
            ws_map = {}

            def g_loadw(off, key):
                ws_map[key] = load_w(off)
                return
                yield

            def proj_gen(lane, key, tt, gain_ap, outs):
                pbank, pbuf = lane_banks[lane]

                def proj():
                    ws = ws_map[key]
                    for kc in range(8):
                        mm(pbank[:, :], GW[ws][:, kc * 128:(kc + 1) * 128], Av(kc, tt * 512, 512), kc == 0, kc == 7,
                           [gwb[ws], Ab[kc][tt]], [pbuf], kc == 7)
                return head_tile(pbank, pbuf, 128, gain_ap, CM[:, C_B128:C_B128 + 128], CM[:, C_R128:C_R128 + 128],
                                 tt * 512, outs, lane, proj, qg_act=True)

            klanes = [[g_loadw(GQA_WK, ("k", 0)), g_loadw(GQA_WK + 1024, ("k", 1))], []]
            for t in range(2):
                for tt in range(4):
                    lane = tt % 2
                    klanes[lane].append(proj_gen(lane, ("k", t), tt, pcol("gqa_k", j),
                                                 [(GK[t][:, tt * 512:(tt + 1) * 512], [gkb[t][tt]], 0, 128)]))
            run_lanes(klanes)
            dma("sp", GWV, wbf[li, :, GQA_WV:GQA_WV + 2048], "gwv", [wreg[(li, "att")]], [gwvb])

            def v_gen():
                for t16 in range(16):
                    bank = psS[(t16 // 2) % 2]
                    hb = t16 % 2
                    bbuf = bS[(t16 // 2) % 2][hb]
                    for kc in range(8):
                        mm(bank[:, hb * 512: hb * 512 + 256], Av(kc, t16 * 128, 128), GWV[:, kc * 256:(kc + 1) * 256], kc == 0, kc == 7,
                           [gwvb, Ab[kc][t16 // 4]], [bbuf], kc == 7)
                    yield
                    for kv in range(4):
                        dst = GV0 + kv * 2048 + t16 * 128 + (kv % 2) * 64
                        copy_("dve", BU[:, dst:dst + 64], bank[:, hb * 512 + kv * 64: hb * 512 + kv * 64 + 64], [bbuf], [gvb[kv]])
                    yield

            qlanes = [[g_loadw(GQA_WQ, ("q", 0))], []]
            for c in range(8):
                if c + 1 < 8:
                    qlanes[0].append(g_loadw(GQA_WQ + (c + 1) * 1024, ("q", c + 1)))
                for tt in range(4):
                    lane = tt % 2
                    qlanes[lane].append(proj_gen(lane, ("q", c), tt, pcol("gqa_q", j),
                                                 [(Bv(c, tt * 512, 512), [Bb[c][tt]], 0, 128)]))
            qlanes.append([v_gen()])
            run_lanes(qlanes)
            iters = [(c, qc, kt) for c in range(8) for qc in range(4) for kt in range(16)]
            obanks = [(psO[0], bO[0]), (psO[1], bO[1]), (psP[0], bP[0]), (psP[1], bP[1])]

            def qk(i):
                c, qc, kt = iters[i]
                s = i % 2
                t = 0 if c < 4 else 1
                for hh in range(2):
                    r0 = hh * 64
                    mm(psS[s][:, hh * 512:(hh + 1) * 512], GK[t][r0:r0 + 64, kt * 128:(kt + 1) * 128],
                       Bv(c, qc * 512, 512, r0, r0 + 64), True, True, [gkb[t][kt // 4], Bb[c][qc]], [bS[s][hh]], hh == 1)

            qk(0)
            for i, (c, qc, kt) in enumerate(iters):
                s = i % 2
                if i + 1 < len(iters):
                    qk(i + 1)
                act(PT[s], psS[s][:, :], AF.Exp, [bS[s][0], bS[s][1]], [ptb[s]], scale=64 ** -0.5)
                u = (i // 16) % 2
                for hh in range(2):
                    kv = (EH[c] if hh == 0 else OH[c]) // 4
                    ob, obuf = obanks[2 * u + hh]
                    vo = GV0 + kv * 2048 + kt * 128
                    mm(ob[:, :], BU[:, vo:vo + 128], PT[s][:, hh * 512:(hh + 1) * 512], kt == 0, kt == 15,
                       [ptb[s], gvb[kv]], [obuf], hh == 1)
                if kt == 15:
                    for hh in range(2):
                        ob, obuf = obanks[2 * u + hh]
                        push_norm(ob, obuf, RDEN[hh], rdenb[hh], hh * 64, Av(c, qc * 512, 512, hh * 64, hh * 64 + 64), [Ab[c][qc]])
                pop_deferred()
            flush_deferred()
            wo_phase(li, GQA_WO, lambda c, t0, n: Av(c, t0, n), Ab)

        MQ = [BU[:, U0 + s * 2048: U0 + (s + 1) * 2048] for s in range(2)]
        MK = [BU[:, U0 + 4096 + s * 2048: U0 + 4096 + (s + 1) * 2048] for s in range(2)]
        MV0 = U0 + 8192
        mqb = [[Buf(f"mq{s}_{tt}") for tt in range(4)] for s in range(2)]
        mkb = [[Buf(f"mk{s}_{tt}") for tt in range(4)] for s in range(2)]
        mvb = [Buf("mv0"), Buf("mv1")]
        WIN = WT[:, 2048:2048 + 5120]
        WKR = WT[:, 7168:7168 + 768]
        winb = Buf("win")
        HW = [WT[:, 7936 + s * 544: 7936 + (s + 1) * 544] for s in range(2)]
        hwb = [Buf("hw0"), Buf("hw1")]

        def mla(li):
            j = li // 2
            if DBG_STOP < 1:
                return
            dma("sp", TAB[:], tab_d[0], "tab", [], [tabb])
            dma("sp", WT[:, 2048:2048 + 5888], wbf[li, :, MLA_WIN:MLA_WIN + 5888], "win", [wreg[(li, "att")]], [winb])
            mla_later = do_norm("mla_norm", j)
            P.op(POOL, lambda e: e.memset(BU[:, MV0:MV0 + 4096], 1.0), [], mvb)
            for s_ in range(2):
                P.op("dve", (lambda e, s_=s_: e.memset(MQ[s_][64:128, :], 0.0)), [], mqb[s_])
                P.op("dve", (lambda e, s_=s_: e.memset(MK[s_][64:128, :], 0.0)), [], mkb[s_])
            banks = [(psS[0][:, 0:512], bS[0][0]), (psS[0][:, 512:1024], bS[0][1]), (psS[1][:, 0:512], bS[1][0]),
                     (psS[1][:, 512:1024], bS[1][1]), (psO[0][:, :], bO[0])]
            for tt in range(4):
                for _ in range(8):
                    if mla_later:
                        mla_later.pop(0)()
                hT_b = [Ab[kc][tt] for kc in range(8)]
                for ch in range(5):
                    bk, bkb = banks[ch]
                    for kc in range(8):
                        mm(bk, WIN[:, kc * 640 + ch * 128: kc * 640 + (ch + 1) * 128], Av(kc, tt * 512, 512), kc == 0, kc == 7,
                           [winb, Ab[kc][tt]], [bkb], kc == 7)
                    s = ch % 2
                    act(SQ[s], bk, AF.Square, [bkb], [sqb[s]])
                    if ch < 3:
                        mm(psP[0][:, :], ones_ap, SQ[s], ch == 0, ch == 2, [sqb[s], cmb], [bP[0]], True)
                    else:
                        mm(psP[1][:, :], ones_ap, SQ[s], ch == 3, ch == 4, [sqb[s], cmb], [bP[1]], True)
                for kc in range(8):
                    mm(psO[1][0:96, :], WKR[:, kc * 96:(kc + 1) * 96], Av(kc, tt * 512, 512), kc == 0, kc == 7,
                       [winb, Ab[kc][tt]], [bO[1]], kc == 7)
                act(RSTD[0], psP[0][:, :], AF.Ln, [bP[0]], [rstdb[0]], scale=1.0 / 384, bias=EPS)
                act(RSTD[0], RSTD[0], AF.Exp, [rstdb[0]], [rstdb[0]], scale=-0.5)
                act(RSTD[1], psP[1][:, :], AF.Ln, [bP[1]], [rstdb[1]], scale=1.0 / 256, bias=EPS)
                act(RSTD[1], RSTD[1], AF.Exp, [rstdb[1]], [rstdb[1]], scale=-0.5)
                for ch in range(5):
                    bk, bkb = banks[ch]
                    if ch < 3:
                        g_ap, r = pcol("q_lora", j, ch), 0
                    else:
                        g_ap, r = pcol("kv_lora", j, ch - 3), 1
                    stt_("dve", Av(ch, tt * 512, 512), bk, g_ap, RSTD[r], [bkb, rstdb[r], parb] , [Ab[ch][tt]] + ([] if ch else hT_b[5:]))
                run(head_tile(psO[1], bO[1], 96, pcol("mla_kr", j, 0, 96), CM[0:96, C_B96:C_B96 + 96],
                              CM[0:96, C_R96:C_R96 + 96], tt * 512,
                              [(MK[0][64:96, tt * 512:(tt + 1) * 512], [mkb[0][tt]], 64, 96),
                               (MK[1][64:96, tt * 512:(tt + 1) * 512], [mkb[1][tt]], 64, 96)], 0, lambda: None))

            if SQ_POOL:
                for (gn, R_, c0) in (("mla_q", 96, 0), ("mla_kn", 64, 128)):
                    gcol_ap = pcol(gn, j, 0, R_)
                    gi = GT[0:R_, (0 if c0 == 0 else 1):(1 if c0 == 0 else 2)]
                    P.op("dve", (lambda e, gi=gi, gcol_ap=gcol_ap: e.reciprocal(out=gi, in_=gcol_ap)), [parb], [gtb])
                    tt_("dve", gi, gi, gi, ALU.mult, [gtb], [gtb])
                    ts_("dve", CMG[0:R_, c0:c0 + R_], CM[0:R_, C_B96:C_B96 + R_], gi, ALU.mult, [cmb, gtb], [cmgb])

            def load_hw(h):
                s = h % 2
                o = MLA_WH + h * 544
                dma("sp", HW[s], wbf[li, :, o:o + 544], f"hw{s}", [wreg[(li, "att")]], [hwb[s]])

            def proj_gens(h):
                s = h % 2

                def g_q(tt):
                    def proj():
                        for kc in range(3):
                            mm(psP[0][0:96, :], HW[s][:, kc * 96:(kc + 1) * 96], Av(kc, tt * 512, 512), kc == 0, kc == 2,
                               [hwb[s], Ab[kc][tt]], [bP[0]], kc == 2)
                    return head_tile(psP[0], bP[0], 96, pcol("mla_q", j, 0, 96),
                                     CMG[0:96, 0:96] if SQ_POOL else CM[0:96, C_B96:C_B96 + 96],
                                     CM[0:96, C_R96:C_R96 + 96], tt * 512,
                                     [(MQ[s][0:96, tt * 512:(tt + 1) * 512], [mqb[s][tt]], 0, 96)], 0, proj, sq_pool=SQ_POOL)

                def g_k(tt):
                    def proj():
                        for kc in range(2):
                            mm(psP[1][0:64, :], HW[s][:, 288 + kc * 64: 288 + (kc + 1) * 64], Av(3 + kc, tt * 512, 512), kc == 0, kc == 1,
                               [hwb[s], Ab[3 + kc][tt]], [bP[1]], kc == 1)
                    return head_tile(psP[1], bP[1], 64, pcol("mla_kn", j, 0, 64),
                                     CMG[0:64, 128:192] if SQ_POOL else CM[0:64, C_B96:C_B96 + 64],
                                     None, tt * 512,
                                     [(MK[s][0:64, tt * 512:(tt + 1) * 512], [mkb[s][tt]], 0, 64)], 1, proj, sq_pool=SQ_POOL)

                def g_v(g8):
                    for i8 in range(8):
                        t16 = g8 * 8 + i8
                        for kc in range(2):
                            mm(psP[1][:, i8 * 64:(i8 + 1) * 64], Av(3 + kc, t16 * 128, 128), HW[s][:, 416 + kc * 64: 416 + (kc + 1) * 64],
                               kc == 0, kc == 1, [hwb[s], Ab[3 + kc][t16 // 4]], [bP[1]], (i8 == 7 and kc == 1))
                    yield
                    vcol = 0 if s == 0 else 64
                    base = MV0 + s * 2048 + g8 * 8 * 128 + vcol
                    dst = BU[:, base: base + 8 * 128].rearrange("p (a b) -> p a b", b=128)[:, :, 0:64]
                    src = psP[1][:, :].rearrange("p (a b) -> p a b", b=64)
                    copy_("dve", dst, src, [bP[1]], [mvb[s]])
                    yield

                lane_a = [g_q(tt) for tt in range(4)]
                lane_b = [g_k(tt) for tt in range(4)] + [g_v(0), g_v(1)]
                return [lane_a, lane_b]

            def unit(h):
                s = h % 2
                c = h // 2
                r0 = 0 if s == 0 else 64
                return dict(
                    q=(lambda qc: MQ[s][:, qc * 512:(qc + 1) * 512]),
                    qb=(lambda qc: [mqb[s][qc]]),
                    k=(lambda kt: MK[s][:, kt * 128:(kt + 1) * 128]),
                    kb=(lambda kt: [mkb[s][kt // 4]]),
                    v=(lambda kt: BU[:, MV0 + s * 2048 + kt * 128: MV0 + s * 2048 + (kt + 1) * 128]),
                    vb=[mvb[s]],
                    top=(s == 0),
                    out=(lambda qc: (Bv(c, qc * 512, 512, r0, r0 + 64), [Bb[c][qc]])),
                )

            if DBG_STOP < 2:
                return
            load_hw(0)
            load_hw(1)
            sp0 = Sprinkle(proj_gens(0))
            sp0.drain()
            for h in range(16):
                spr = Sprinkle(proj_gens(h + 1) if h + 1 < 16 else [], every=1)
                if h + 2 < 16:
                    load_hw(h + 2)
                attention([unit(h)], spr, 96 ** -0.5)
            flush_deferred()
            if DBG_STOP < 3:
                return
            wo_phase(li, MLA_WO, lambda c, t0, n: Bv(c, t0, n), Bb)

        P.barrier([(("d", "cm"), 16), (("d", "par"), 16)])
        for sq_i in range(n_seq):
            for kc in range(8):
                dma("sp", xT[:, kc * S:(kc + 1) * S], xin[sq_i, kc * 128:(kc + 1) * 128, :], f"xl{kc}", [], xb[kc])
            for li in layers:
                if li % 2 == 0:
                    mla(li)
                else:
                    gqa(li)
                P.barrier()
                last = (li == layers[-1])
                if DBG_STOP >= 4:
                    ffn(li, INCR_NORM and not last, sq_i if last else None)
                P.barrier()
            if not layers or DBG_STOP < 4:
                for kc in range(8):
                    dma("sp", yout[sq_i, kc * 128:(kc + 1) * 128, :], xT[:, kc * S:(kc + 1) * S], f"xs{kc}", xb[kc], [])
        P.wait_only("sp", [(("d", f"xs{kc}"), P.dcnt[("d", f"xs{kc}")]) for kc in range(8)])

        sems = {}
        for k in P.inckeys:
            sems[k] = es.enter_context(nc.semaphore("s_" + "_".join(str(t) for t in k)))
        with nc.Block() as block:
            def replay(e, ops):
                for (waits, fn, inc, amt) in ops:
                    for (k, v) in waits:
                        e.wait_ge(sems[k], v)
                    if fn is not None:
                        ins = fn(e)
                        if inc is not None:
                            ins.then_inc(sems[inc], amt)

            @block.sync
            def _(e):
                replay(e, P.ops["sp"])

            @block.tensor
            def _(e):
                replay(e, P.ops["pe"])

            @block.scalar
            def _(e):
                replay(e, P.ops["act"])

            @block.vector
            def _(e):
                replay(e, P.ops["dve"])

            @block.gpsimd
            def _(e):
                replay(e, P.ops["pool"])
    return nc, P


_CONSTS = None


def kernel(**inputs):
    global _CONSTS
    inp = {k: np.asarray(v) for k, v in inputs.items()}
    x = inp["x"].astype(np.float32, copy=False)
    wb, par = _host_layout(inp)
    if _CONSTS is None:
        _CONSTS = _host_consts()
    cm, tab = _CONSTS
    tab2 = np.ascontiguousarray(tab.reshape(2, 128, 2 * S))
    nseq = x.shape[0] // NCORE
    nc, _ = build_nc(nseq, (0, 1, 2, 3))
    in_maps = []
    for c in range(NCORE):
        xs = np.ascontiguousarray(x[c * nseq:(c + 1) * nseq].transpose(0, 2, 1))
        in_maps.append({"xT": xs, "wblob": wb, "params": par, "cmat": cm, "tabs": tab2})
    res = run_bass_kernel_spmd(nc, in_maps, core_ids=list(range(NCORE)))
    outs = [np.asarray(r["yT"]).transpose(0, 2, 1) for r in res.results]
    return np.ascontiguousarray(np.concatenate(outs, axis=0)).astype(np.float32, copy=False)
```
